# Optimizing a Trainium2 kernel written in Bass

```python
import math
import jax, jax.numpy as jnp
from jax import lax
import numpy as np

D_MODEL = 1024
BATCH = 4
SEQ = 4096
DEPTH = 2

N_A = DEPTH // 2
N_B = DEPTH - N_A
CONV_W = 3
HEAD_DIM = 64
N_HEADS = D_MODEL // HEAD_DIM
N_KV_HEADS = 4
GROUP = N_HEADS // N_KV_HEADS
WINDOW = 128
BLOCK = 128
ROT_DIM = HEAD_DIM // 4
ROPE_THETA = 500000.0
D_FF = ((8 * D_MODEL // 3 + 255) // 256) * 256
EPS = 1e-6
NEG = -1e30

kernel_name = "yoco_shortconv_swa_sink_hybrid"


def rmsnorm(x, g):
    xf = x.astype(jnp.float32)
    r = lax.rsqrt(jnp.mean(xf * xf, axis=-1, keepdims=True) + EPS)
    return (xf * r).astype(x.dtype) * g


def rope_tables(seq_len, dtype):
    inv_freq = ROPE_THETA ** (-jnp.arange(0, ROT_DIM, 2, dtype=jnp.float32) / ROT_DIM)
    ang = jnp.arange(seq_len, dtype=jnp.float32)[:, None] * inv_freq[None, :]
    return jnp.cos(ang)[:, None, :].astype(dtype), jnp.sin(ang)[:, None, :].astype(dtype)


def partial_rotary(t, cos, sin):
    half = ROT_DIM // 2
    t1, t2, rest = t[..., :half], t[..., half:ROT_DIM], t[..., ROT_DIM:]
    return jnp.concatenate([t1 * cos - t2 * sin, t2 * cos + t1 * sin, rest], axis=-1)


def causal_depthwise_conv3(u, w):
    s = u.shape[1]
    up = jnp.pad(u, ((0, 0), (CONV_W - 1, 0), (0, 0)))
    return up[:, 0:s] * w[0] + up[:, 1:s + 1] * w[1] + up[:, 2:s + 2] * w[2]


def short_conv_mixer(h, w_in, conv_w, w_out):
    bcx = h @ w_in
    b_gate, c_gate, u = jnp.split(bcx, 3, axis=-1)
    y = b_gate * causal_depthwise_conv3(c_gate * u, conv_w)
    return y @ w_out


def swiglu(h, w_gate_up, w_down):
    g, u = jnp.split(h @ w_gate_up, 2, axis=-1)
    return (jax.nn.silu(g) * u) @ w_down


def sliding_window_sink_attention(q, k, v, sinks):
    bsz, s = q.shape[0], q.shape[1]
    nb = s // BLOCK
    qb = q.reshape(bsz, nb, BLOCK, N_KV_HEADS, GROUP, HEAD_DIM)

    def with_prev(t):
        tb = t.reshape(bsz, nb, BLOCK, N_KV_HEADS, HEAD_DIM)
        prev = jnp.concatenate([jnp.zeros_like(tb[:, :1]), tb[:, :-1]], axis=1)
        return jnp.concatenate([prev, tb], axis=2)

    kk, vv = with_prev(k), with_prev(v)
    scale = 1.0 / math.sqrt(HEAD_DIM)
    scores = jnp.einsum('bnqhgd,bnkhd->bnhgqk', qb, kk).astype(jnp.float32) * scale

    qi = jnp.arange(BLOCK)[:, None]
    kj = jnp.arange(2 * BLOCK)[None, :]
    diff = BLOCK + qi - kj
    band = (diff >= 0) & (diff < WINDOW)
    not_pad = (jnp.arange(nb)[:, None, None] > 0) | (kj[None] >= BLOCK)
    valid = band[None] & not_pad
    scores = jnp.where(valid[None, :, None, None], scores, NEG)

    sink = jnp.broadcast_to(
        sinks.astype(jnp.float32).reshape(N_KV_HEADS, GROUP)[None, None, :, :, None, None],
        scores.shape[:-1] + (1,))
    probs = jax.nn.softmax(jnp.concatenate([scores, sink], axis=-1), axis=-1)[..., :-1]
    out = jnp.einsum('bnhgqk,bnkhd->bnqhgd', probs.astype(v.dtype), vv)
    return out.reshape(bsz, s, N_HEADS * HEAD_DIM)


def setup_inputs(seed: int = 0) -> dict:
    key = jax.random.key(seed)
    ks = jax.random.split(key, 24)
    f32 = jnp.float32
    D, F = D_MODEL, D_FF
    QD = N_HEADS * HEAD_DIM
    KVD = N_KV_HEADS * HEAD_DIM

    def nrm(k, shape, fan_in):
        return jax.random.normal(k, shape, f32) * (fan_in ** -0.5)

    def gain(k, shape):
        return 1.0 + 0.05 * jax.random.normal(k, shape, f32)

    return {
        "x": jax.random.normal(ks[0], (BATCH, SEQ, D), f32),
        "a_pre_norm": gain(ks[1], (N_A, D)),
        "a_w_in": nrm(ks[2], (N_A, D, 3 * D), D),
        "a_conv_w": nrm(ks[3], (N_A, CONV_W, D), CONV_W),
        "a_w_out": nrm(ks[4], (N_A, D, D), D),
        "a_post_norm": gain(ks[5], (N_A, D)),
        "ffn_pre_norm": gain(ks[6], (DEPTH, D)),
        "ffn_w_gate_up": nrm(ks[7], (DEPTH, D, 2 * F), D),
        "ffn_w_down": nrm(ks[8], (DEPTH, F, D), F),
        "ffn_post_norm": gain(ks[9], (DEPTH, D)),
        "kv_norm": gain(ks[10], (D,)),
        "w_kv": nrm(ks[11], (D, 2 * KVD), D),
        "b_pre_norm": gain(ks[12], (N_B, D)),
        "b_w_q": nrm(ks[13], (N_B, D, QD), D),
        "b_sinks": 0.5 * jax.random.normal(ks[14], (N_B, N_HEADS), f32),
        "b_w_o": nrm(ks[15], (N_B, QD, D), QD),
        "b_post_norm": gain(ks[16], (N_B, D)),
    }


def reference(x, a_pre_norm, a_w_in, a_conv_w, a_w_out, a_post_norm,
              ffn_pre_norm, ffn_w_gate_up, ffn_w_down, ffn_post_norm,
              kv_norm, w_kv,
              b_pre_norm, b_w_q, b_sinks, b_w_o, b_post_norm):
    bsz, s, _ = x.shape
    cos, sin = rope_tables(s, x.dtype)
    h = x
    for l in range(DEPTH):
        if l < N_A:
            mix = short_conv_mixer(rmsnorm(h, a_pre_norm[l]), a_w_in[l], a_conv_w[l], a_w_out[l])
            h = h + rmsnorm(mix, a_post_norm[l])
        else:
            j = l - N_A
            if j == 0:
                kv = rmsnorm(h, kv_norm) @ w_kv
                k_sh, v_sh = jnp.split(kv, 2, axis=-1)
                k_sh = partial_rotary(k_sh.reshape(bsz, s, N_KV_HEADS, HEAD_DIM), cos, sin)
                v_sh = v_sh.reshape(bsz, s, N_KV_HEADS, HEAD_DIM)
            q = (rmsnorm(h, b_pre_norm[j]) @ b_w_q[j]).reshape(bsz, s, N_HEADS, HEAD_DIM)
            q = partial_rotary(q, cos, sin)
            attn = sliding_window_sink_attention(q, k_sh, v_sh, b_sinks[j]) @ b_w_o[j]
            h = h + rmsnorm(attn, b_post_norm[j])
        ff = swiglu(rmsnorm(h, ffn_pre_norm[l]), ffn_w_gate_up[l], ffn_w_down[l])
        h = h + rmsnorm(ff, ffn_post_norm[l])
    return h
```

```python
import math
import numpy as np
import concourse.bass as bass
import concourse.mybir as mybir
from concourse.bass_utils import run_bass_kernel_spmd

F32 = mybir.dt.float32
BF16 = mybir.dt.bfloat16
ALU = mybir.AluOpType
AF = mybir.ActivationFunctionType

D = 1024
DFF = 2816
FC = DFF // 128
SEQ = 4096
BATCH = 4
NCORES = 8
TOK = 2048
XT = TOK + 130
W = 642
EPS = 1e-6
ROPE_THETA = 500000.0
NS = 5
ATTN_DBG = "11"
SLOT = 4096

TILES = [
    dict(n0=642, subs=[(0, 321), (321, 321)], kv0=2, kvp=[(0, 320), (320, 320)], nkb=5, q0=130, x0=0),
    dict(n0=512, subs=[(0, 512)], kv0=0, kvp=[(0, 512)], nkb=4, q0=0, x0=642),
    dict(n0=512, subs=[(0, 512)], kv0=0, kvp=[(0, 512)], nkb=4, q0=0, x0=1154),
    dict(n0=512, subs=[(0, 512)], kv0=0, kvp=[(0, 512)], nkb=4, q0=0, x0=1666),
]

P_A_PRE, P_A_POST, P_F_PRE0, P_F_POST0, P_KVN, P_B_PRE, P_B_POST, P_F_PRE1, P_F_POST1 = [8 * i for i in range(9)]
P_CW = 72
P_SINK = 96
NPRM = 112

UNITS = ([("in%d" % f, 3072) for f in range(8)] + [("out%d" % j, 4096) for j in range(2)] +
         [("gu0_%d" % j, 4096) for j in range(11)] + [("dn0_%d" % m, 2816) for m in range(8)] +
         [("k", 4096), ("v", 2048)] + [("q%d" % j, 4096) for j in range(2)] + [("o%d" % j, 4096) for j in range(2)] +
         [("gu1_%d" % j, 4096) for j in range(11)] + [("dn1_%d" % m, 2816) for m in range(8)])
UOFF = {}
_o = 0
for _n, _c in UNITS:
    UOFF[_n] = (_o, _c)
    _o += _c
TOTC = _o


class _Unit:
    __slots__ = ("lw", "rd")

    def __init__(self):
        self.lw = None
        self.rd = []


class _Op:
    __slots__ = ("eng", "fn", "deps", "needed", "sem", "inc", "sig", "clock", "eidx", "isdma")


class Sched:
    ENGS = ("pe", "act", "dve", "pool", "sp")

    def __init__(self, nc, ndma_sems=16):
        self.nc = nc
        self.eng = {"pe": nc.tensor, "act": nc.scalar, "dve": nc.vector, "pool": nc.gpsimd, "sp": nc.sync}
        self.ops = []
        self.ecount = {e: 0 for e in self.ENGS}
        self.esem = {e: nc.alloc_semaphore("s_" + e) for e in self.ENGS}
        self.dpool, self.dpos, self.dlast = {}, {}, {}
        for q in ("sp", "pool"):
            self.dpool[q] = [nc.alloc_semaphore("d_%s%d" % (q, i)) for i in range(ndma_sems)]
            self.dpos[q] = 0
            self.dlast[q] = [None] * ndma_sems
        self.units = {}

    def _U(self, xs, out):
        for x in xs:
            if isinstance(x, list):
                self._U(x, out)
            else:
                u = self.units.get(x)
                if u is None:
                    u = _Unit()
                    self.units[x] = u
                out.append(u)
        return out

    def op(self, eng, fn, reads=(), writes=(), dma=0):
        reads = self._U(reads, [])
        writes = self._U(writes, [])
        o = _Op()
        o.eng, o.fn, o.needed, o.isdma = eng, fn, False, dma > 0
        o.inc = 16 * dma if dma else 1
        o.sig = o.clock = None
        o.eidx = self.ecount[eng]
        self.ecount[eng] += 1
        deps = []
        for u in reads:
            if u.lw is not None:
                deps.append(u.lw)
        for u in writes:
            if u.lw is not None:
                deps.append(u.lw)
            deps.extend(u.rd)
        if dma:
            i = self.dpos[eng]
            self.dpos[eng] = (i + 1) % len(self.dpool[eng])
            o.sem = self.dpool[eng][i]
            if self.dlast[eng][i] is not None:
                deps.append(self.dlast[eng][i])
            self.dlast[eng][i] = o
            o.needed = True
        else:
            o.sem = self.esem[eng]
        fd, seen = [], set()
        for d in deps:
            if d is o or id(d) in seen:
                continue
            seen.add(id(d))
            if (not d.isdma) and d.eng == eng:
                if eng in ("pe", "sp"):
                    continue
                if o.eidx - d.eidx > 4:
                    continue
            fd.append(d)
            d.needed = True
        o.deps = fd
        ws = set()
        for u in writes:
            u.lw = o
            u.rd = []
            ws.add(id(u))
        for u in reads:
            if id(u) in ws:
                continue
            if not o.isdma:
                u.rd = [r for r in u.rd if r.isdma or r.eng != eng]
            u.rd.append(o)
        self.ops.append(o)
        return o

    def emit(self):
        clock = {e: {} for e in self.ENGS}
        semval = {}
        for o in self.ops:
            ck = clock[o.eng]
            waits = {}
            for d in o.deps:
                s, v = d.sig
                if ck.get(id(s), (None, 0))[1] >= v:
                    continue
                if id(s) not in waits or waits[id(s)][1] < v:
                    waits[id(s)] = (s, v)
            for d in o.deps:
                for k, sv in d.clock.items():
                    if ck.get(k, (None, 0))[1] < sv[1]:
                        ck[k] = sv
            wl = list(waits.values())
            e = self.eng[o.eng]
            for (s, v) in wl[1:]:
                e.wait_ge(s, v)
            ins = o.fn(e)
            if wl:
                ins._wait_ge(wl[0][0], wl[0][1])
            if o.needed:
                v = semval.get(id(o.sem), 0) + o.inc
                semval[id(o.sem)] = v
                o.sig = (o.sem, v)
                ins.then_inc(o.sem, o.inc)
                c = dict(ck)
                c[id(o.sem)] = (o.sem, v)
                o.clock = c
            o.fn = None
            o.deps = None


def build_program(dbg=None):
    nc = bass.Bass("TRN2", target_bir_lowering=False)
    xT = nc.dram_tensor("xT", [128, 8, XT], F32, kind="ExternalInput")
    wst = nc.dram_tensor("wst", [128, TOTC], F32, kind="ExternalInput")
    prm_d = nc.dram_tensor("prm", [128, NPRM], F32, kind="ExternalInput")
    rope_d = nc.dram_tensor("rope", [128, 2, TOK + 128], F32, kind="ExternalInput")
    msk_d = nc.dram_tensor("msk", [128, 3, 512], F32, kind="ExternalInput")
    rm_d = nc.dram_tensor("rm", [128, 128], F32, kind="ExternalInput")
    oT = nc.dram_tensor("oT", [128, 8, TOK], F32, kind="ExternalOutput")

    S = Sched(nc)
    A = nc.alloc_sbuf_tensor
    hb = [A("h0", [128, 8, W], F32), A("h1", [128, 8, W], F32)]
    xn = A("xn", [128, 8, W], BF16)
    xnB = A("xnB", [128, 8, 512], BF16)
    sq = A("sq", [128, 8, W], BF16)
    yT = A("yT", [128, 8, W], BF16)
    mix = A("mix", [128, 8, W], F32)
    aT = A("aT", [128, FC, W], BF16)
    rtb = [A("rt0", [128, W], F32), A("rt1", [128, W], F32)]
    cu = A("cu", [128, W + 2], F32)
    usb = [A("usb%d" % i, [128, 512], F32) for i in range(2)]
    tmpf = [A("tmpf%d" % i, [128, 512], F32) for i in range(3)]
    sgb = [A("sg%d" % i, [128, 512], BF16) for i in range(2)]
    carry = A("carry", [128, 8, 2], F32)
    ring = A("ring", [128, NS, SLOT], BF16)
    kraw = [A("kraw%d" % i, [128, 512], BF16) for i in range(2)]
    KT = A("KT", [128, 4, 640], BF16)
    Vb = A("Vb", [128, 5, 4, 64], BF16)
    esg = A("esg", [128, 8], F32)
    ropet = A("ropet", [128, 2, 640], F32)
    msk = A("mskb", [128, 3, 512], BF16)
    rm = A("rmb", [128, 128], BF16)
    ones = A("ones", [128, 128], BF16)
    prm = A("prmb", [128, NPRM], F32)
    es = A("es", [128, 16], F32)
    epsb = A("epsb", [128, 1], F32)
    rd = [A("rd%d" % i, [128, 256], F32) for i in range(2)]
    pst = nc.alloc_psum_tensor("pst", [128, 8, 512], F32)
    ps = [pst[:, i, :] for i in range(8)]

    cnt = {"ps": 0, "w": 0, "rt": 0, "usb": 0, "tmp": 0, "sg": 0, "kraw": 0, "rd": 0, "pt": 0, "pm": 0}

    def rot(name, n):
        i = cnt[name]
        cnt[name] = (i + 1) % n
        return i

    def nb():
        i = rot("ps", 8)
        return ("ps", i), ps[i]

    def nb2():
        i = cnt["ps"]
        i = (i + (i % 2)) % 8
        cnt["ps"] = (i + 2) % 8
        return [("ps", i), ("ps", i + 1)], i

    def MM(out, lhsT, rhs, st, sp, reads, writes):
        S.op("pe", lambda e: e.matmul(out, lhsT, rhs, start=st, stop=sp), reads, writes)

    def ACT(out, in_, func, reads, writes, **kw):
        S.op("act", lambda e: e.activation(out=out, in_=in_, func=func, **kw), reads, writes)

    def ACOPY(out, in_, reads, writes):
        S.op("act", lambda e: e.copy(out=out, in_=in_), reads, writes)

    def TT(out, in0, in1, op, reads, writes):
        S.op("dve", lambda e: e.tensor_tensor(out=out, in0=in0, in1=in1, op=op), reads, writes)

    def STT(out, in0, scalar, in1, op0, op1, reads, writes):
        S.op("dve", lambda e: e.scalar_tensor_tensor(out=out, in0=in0, scalar=scalar, in1=in1, op0=op0, op1=op1), reads, writes)

    def TSM(out, in0, scalar, reads, writes):
        S.op("dve", lambda e: e.tensor_scalar(out=out, in0=in0, scalar1=scalar, scalar2=None, op0=ALU.mult), reads, writes)

    def RECIP(out, in_, reads, writes):
        S.op("dve", lambda e: e.reciprocal(out=out, in_=in_), reads, writes)

    def DMA(q, out, in_, reads, writes):
        S.op(q, lambda e: e.dma_start(out=out, in_=in_), reads, writes, dma=1)

    DMA("sp", prm[:, :], prm_d.ap(), [], ["prm"])
    DMA("pool", msk[:, :, :], msk_d.ap(), [], ["msk"])
    DMA("pool", rm[:, :], rm_d.ap(), [], ["rm"])
    S.op("dve", lambda e: e.memset(ones[:, :], 1.0), writes=["ones"])
    S.op("dve", lambda e: e.memset(epsb[:, :], EPS), writes=["epsb"])
    S.op("dve", lambda e: e.memset(carry[:, :, :], 0.0), writes=["carry"])
    ACT(es[:, :], prm[:, P_SINK:P_SINK + 16], AF.Exp, ["prm"], ["es"])
    for e2 in range(2):
        ACOPY(esg[64 * e2:64 * e2 + 64, :].rearrange("p (g j) -> p g j", g=4),
              es[64 * e2:64 * e2 + 64, :].rearrange("p (g j e) -> p g j e", g=4, j=2, e=2)[:, :, :, e2], ["es"], ["esg"])

    wstate = {"issued": 0}
    U_CONV = ["in%d" % f for f in range(8)] + ["out0", "out1"]
    U_FFN = [["gu%d_%d" % (l, j) for j in range(11)] + ["dn%d_%d" % (l, m) for m in range(8)] for l in range(2)]
    U_ATT = ["v", "k", "q0", "q1", "o0", "o1"]
    NT = len(TILES)
    if dbg == "L0":
        seq = (U_CONV + U_FFN[0]) * NT
    else:
        seq = U_CONV + U_FFN[0]
        for i in range(NT):
            nx_ = i + 1 < NT
            if i % 2 == 1:
                seq = seq + U_ATT + (U_CONV if nx_ else []) + U_FFN[1] + (U_FFN[0] if nx_ else [])
            else:
                seq = seq + (U_CONV if nx_ else []) + U_ATT + (U_FFN[0] if nx_ else []) + U_FFN[1]

    def wnext(name):
        j = cnt["w"]
        cnt["w"] = j + 1
        assert seq[j] == name, (seq[j], name)
        while wstate["issued"] < min(len(seq), j + NS):
            i = wstate["issued"]
            off, ncol = UOFF[seq[i]]
            sl = i % NS
            hold = [("h0", k) for k in range(8)] if (1 <= i < NS and dbg != "attn") else []
            DMA("pool", ring[:, sl, 0:ncol], wst.ap()[:, off:off + ncol], hold, [("ring", sl)])
            wstate["issued"] = i + 1
        sl = j % NS
        return ("ring", sl), sl

    def gcol(base, k):
        return prm[:, base + k:base + k + 1]

    def pipeline(jobs, s1, s2):
        prev = None
        for j in jobs:
            st_ = s1(j)
            if prev is not None:
                s2(*prev)
            prev = (j, st_)
        if prev is not None:
            s2(*prev)

    def rstats(so, n, rbuf, runit):
        pu, p = nb()
        for k in range(8):
            MM(p[:, 0:n], ones[:, :], sq[:, k, so:so + n], k == 0, k == 7, [("sq", k), "ones"], [pu])
        ACT(rbuf[:, so:so + n], p[:, 0:n], AF.Ln, [pu, "epsb"], [runit], bias=epsb[:, 0:1], scale=1.0 / D)
        ACT(rbuf[:, so:so + n], rbuf[:, so:so + n], AF.Exp, [runit], [runit], scale=-0.5)

    def norm_apply(h, hname, so, n, gb, ob, oname, ooff, rbuf, runit):
        for k in range(8):
            STT(ob[:, k, so - ooff:so - ooff + n], h[:, k, so:so + n], gcol(gb, k), rbuf[:, so:so + n], ALU.mult, ALU.mult,
                [(hname, k), "prm", runit], [(oname, k)])

    def prenorm(h, hname, subs, specs, late=()):
        ri = rot("rt", 2)
        rbuf, runit = rtb[ri], ("rt", ri)
        for (so, n) in subs:
            for k in range(8):
                ACT(sq[:, k, so:so + n], h[:, k, so:so + n], AF.Square, [(hname, k)], [("sq", k)])
            rstats(so, n, rbuf, runit)
            for (gb, ob, oname, ooff) in specs:
                norm_apply(h, hname, so, n, gb, ob, oname, ooff, rbuf, runit)
        for (gb, ob, oname, ooff, so, n) in late:
            norm_apply(h, hname, so, n, gb, ob, oname, ooff, rbuf, runit)

    def proj_postnorm(h, hname, subs, src, sname, nk, unames, per_unit, gbase, to_mix=False, hook=None, tail_to=None):
        wu = sl = None
        for m in range(8):
            if hook is not None:
                hook(("dn", m))
            if m % per_unit == 0:
                wu, sl = wnext(unames[m // per_unit])
            ml = m % per_unit
            for (so, n) in subs:
                pu, p = nb()
                for k in range(nk):
                    c0 = (ml * nk + k) * 128
                    MM(p[:, 0:n], ring[:, sl, c0:c0 + 128], src[:, k, so:so + n], k == 0, k == nk - 1, [wu, (sname, k)], [pu])
                ACOPY(mix[:, m, so:so + n], p[:, 0:n], [pu], [("mix", m)])
                ACT(sq[:, m, so:so + n], p[:, 0:n], AF.Square, [pu], [("sq", m)])
        def tail():
            ri = rot("rt", 2)
            rbuf, runit = rtb[ri], ("rt", ri)
            for (so, n) in subs:
                rstats(so, n, rbuf, runit)
                for k in range(8):
                    ti_ = rot("tmp", 3)
                    t = tmpf[ti_]
                    STT(t[:, 0:n], mix[:, k, so:so + n], gcol(gbase, k), rbuf[:, so:so + n], ALU.mult, ALU.mult,
                        [("mix", k), "prm", runit], [("tmp", ti_)])
                    if to_mix:
                        TT(mix[:, k, so:so + n], h[:, k, so:so + n], t[:, 0:n], ALU.add, [("tmp", ti_), (hname, k), ("mix", k)], [("mix", k)])
                    else:
                        TT(h[:, k, so:so + n], h[:, k, so:so + n], t[:, 0:n], ALU.add, [("tmp", ti_), (hname, k)], [(hname, k)])

        if tail_to is None:
            tail()
        else:
            tail_to.append(tail)

    def ffn_pre(h, hname, subs, g_pre, xb, xname, xoff):
        prenorm(h, hname, subs, [(g_pre, xb, xname, xoff)])

    def ffn_main(h, hname, subs, layer, g_post, xb, xname, xoff, to_mix=False, hook=None, tail_to=None):
        for j in range(11):
            if hook is not None:
                hook(("gu", j))
            wu, sl = wnext("gu%d_%d" % (layer, j))
            for fl in range(2):
                f = 2 * j + fl
                for (so, n) in subs:
                    pgu, pg = nb()
                    puu, pup = nb()
                    for gi, (pp, ppu) in enumerate(((pg, pgu), (pup, puu))):
                        for k in range(8):
                            c0 = ((fl * 2 + gi) * 8 + k) * 128
                            MM(pp[:, 0:n], ring[:, sl, c0:c0 + 128], xb[:, k, so - xoff:so - xoff + n], k == 0, k == 7, [wu, (xname, k)], [ppu])
                    si = rot("sg", 2)
                    sg = sgb[si]
                    ACT(sg[:, 0:n], pg[:, 0:n], AF.Silu, [pgu], [("sg", si)])
                    TT(aT[:, f, so:so + n], pup[:, 0:n], sg[:, 0:n], ALU.mult, [puu, ("sg", si)], [("aT", f)])
        proj_postnorm(h, hname, subs, aT, "aT", FC, ["dn%d_%d" % (layer, m) for m in range(8)], 1, g_post, to_mix=to_mix, hook=hook, tail_to=tail_to)

    def conv_pre(h, hname, T):
        prenorm(h, hname, T["subs"], [(P_A_PRE, xn, "xn", 0)])

    def conv_main(h, hname, T, mid=None, tail_to=None):
        subs = T["subs"]
        n0 = T["n0"]
        for f in range(8):
            if f == 4 and mid is not None:
                mid()
            wu, sl = wnext("in%d" % f)
            ACOPY(cu[:, 0:2], carry[:, f, :], ["carry"], ["cu"])
            for (so, n) in subs:
                banks = [nb() for _ in range(3)]
                for j, (pu, p) in enumerate(banks):
                    for k in range(8):
                        c0 = k * 384 + j * 128
                        MM(p[:, 0:n], ring[:, sl, c0:c0 + 128], xn[:, k, so:so + n], k == 0, k == 7, [wu, ("xn", k)], [pu])
                (pbu, pb), (pcu, pc), (puu, pup) = banks
                ui = rot("usb", 2)
                ub = usb[ui]
                ACOPY(ub[:, 0:n], pup[:, 0:n], [puu], [("usb", ui)])
                TT(cu[:, 2 + so:2 + so + n], pc[:, 0:n], ub[:, 0:n], ALU.mult, [pcu, ("usb", ui)], ["cu"])
                t0i = rot("tmp", 3)
                t1i = rot("tmp", 3)
                t0, t1 = tmpf[t0i], tmpf[t1i]
                TSM(t0[:, 0:n], cu[:, 2 + so:2 + so + n], gcol(P_CW + 16, f), ["cu", "prm"], [("tmp", t0i)])
                STT(t1[:, 0:n], cu[:, 1 + so:1 + so + n], gcol(P_CW + 8, f), t0[:, 0:n], ALU.mult, ALU.add,
                    ["cu", "prm", ("tmp", t0i)], [("tmp", t1i)])
                STT(t0[:, 0:n], cu[:, so:so + n], gcol(P_CW, f), t1[:, 0:n], ALU.mult, ALU.add,
                    ["cu", "prm", ("tmp", t1i)], [("tmp", t0i)])
                TT(yT[:, f, so:so + n], pb[:, 0:n], t0[:, 0:n], ALU.mult, [pbu, ("tmp", t0i)], [("yT", f)])
            ACOPY(carry[:, f, :], cu[:, n0:n0 + 2], ["cu"], ["carry"])
        proj_postnorm(h, hname, subs, yT, "yT", 8, ["out0", "out1"], 4, P_A_POST, tail_to=tail_to)

    def rope_evac(pu, p, n, tab0, dst, dunits):
        ki = rot("kraw", 2)
        kr = kraw[ki]
        ACOPY(kr[:, 0:n], p[:, 0:n], [pu], [("kraw", ki)])
        p2u, p2 = nb()
        MM(p2[:, 0:n], rm[:, :], kr[:, 0:n], True, True, ["rm", ("kraw", ki)], [p2u])
        t0i = rot("tmp", 3)
        t1i = rot("tmp", 3)
        t0, t1 = tmpf[t0i], tmpf[t1i]
        TT(t0[:, 0:n], p[:, 0:n], ropet[:, 0, tab0:tab0 + n], ALU.mult, [pu, "ropet", ("kraw", ki)], [("tmp", t0i)])
        TT(t1[:, 0:n], p2[:, 0:n], ropet[:, 1, tab0:tab0 + n], ALU.mult, [p2u, "ropet"], [("tmp", t1i)])
        TT(dst, t0[:, 0:n], t1[:, 0:n], ALU.add, [("tmp", t0i), ("tmp", t1i)], dunits)

    def attn_pre(h, hname, T, ti, xkvb, xkvn):
        kv0, q0, nkb = T["kv0"], T["q0"], T["nkb"]
        kvl0 = T["x0"] + kv0 - 2
        nkv = nkb * 128
        DMA("sp", ropet[:, :, 0:nkv], rope_d.ap()[:, :, kvl0:kvl0 + nkv], [], ["ropet"])
        if ti > 0:
            ACOPY(KT[:, :, 0:128], KT[:, :, 512:640], [("KT", g, 4) for g in range(4)], [("KT", g, 0) for g in range(4)])
            ACOPY(Vb[:, 0, :, :], Vb[:, 4, :, :], [("Vb", 4)], [("Vb", 0)])
        ri = rot("rt", 2)
        rbuf, runit = rtb[ri], ("rt", ri)
        for (so, n) in T["subs"]:
            for k in range(8):
                ACT(sq[:, k, so:so + n], h[:, k, so:so + n], AF.Square, [(hname, k)], [("sq", k)])
            rstats(so, n, rbuf, runit)
            norm_apply(h, hname, so, n, P_KVN, xkvb, xkvn, 0, rbuf, runit)
        return rbuf, runit

    def attn_pre_q(h, hname, T, rr):
        q0 = T["q0"]
        norm_apply(h, hname, q0, 512, P_B_PRE, xnB, "xnB", q0, rr[0], rr[1])

    def attn_main(h, hname, T, ti, xkv, xkvn, mid=None, pre_q=None, tail_to=None):
        n0, kv0, q0, nkb = T["n0"], T["kv0"], T["q0"], T["nkb"]
        ks0 = 5 - nkb
        tq0 = q0 - kv0
        if dbg == "attn":
            return attn_core(T, ti, int(ATTN_DBG[0]), int(ATTN_DBG[1]))
        wu, sl = wnext("v")
        for bi in range(nkb):
            pu, p = nb()
            t0 = kv0 + bi * 128
            for k in range(8):
                MM(p[:, 0:256], xkv[:, k, t0:t0 + 128], ring[:, sl, k * 256:(k + 1) * 256], k == 0, k == 7, [wu, (xkvn, k)], [pu])
            vs = ks0 + bi
            ACOPY(Vb[:, vs, :, :], p[:, 0:256].rearrange("p (g d) -> p g d", g=4), [pu], [("Vb", vs)])
        wu, sl = wnext("k")
        kjobs = []
        for g in range(4):
            for (po, pn) in T["kvp"]:
                kjobs.append((g, po, pn, wu, sl))

        def k_s1(job):
            g, po, pn, wu_, sl_ = job
            pu, p = nb()
            for k in range(8):
                c0 = (g * 8 + k) * 128
                MM(p[:, 0:pn], ring[:, sl_, c0:c0 + 128], xkv[:, k, kv0 + po:kv0 + po + pn], k == 0, k == 7, [wu_, (xkvn, k)], [pu])
            return (pu, p)

        def k_s2(job, st_):
            g, po, pn, wu_, sl_ = job
            pu, p = st_
            c1 = ks0 * 128 + po
            rope_evac(pu, p, pn, po, KT[:, g, c1:c1 + pn], [("KT", g, s_) for s_ in range(5)])

        pipeline(kjobs, k_s1, k_s2)
        if pre_q is not None:
            pre_q()
        if mid is not None:
            mid()
        def q_s1(c):
            if c % 4 == 0:
                qst["w"] = wnext("q%d" % (c // 4))
            wu_, sl_ = qst["w"]
            pu, p = nb()
            for k in range(8):
                c0 = ((c % 4) * 8 + k) * 128
                MM(p[:, 0:512], ring[:, sl_, c0:c0 + 128], xnB[:, k, 0:512], k == 0, k == 7, [wu_, ("xnB", k)], [pu])
            return (pu, p)

        def q_s2(c, st_):
            pu, p = st_
            rope_evac(pu, p, 512, tq0, aT[:, 8 + c, 0:512], [("aT", 8 + c)])

        qst = {}
        pipeline(list(range(8)), q_s1, q_s2)
        attn_core(T, ti, 4, 4)
        proj_postnorm(h, hname, [(q0, 512)], yT, "yT", 8, ["o0", "o1"], 4, P_B_POST, tail_to=tail_to)

    def attn_core(T, ti, nqb, ng):
        q0 = T["q0"]
        def a_s1(job):
            qb, g = job
            qc = qb * 128
            su, bi0 = nb2()
            kslots = (qb, qb + 1)
            for kbi, kslot in enumerate(kslots):
                for e2 in range(2):
                    MM(pst[:, bi0 + e2, kbi * 256:(kbi + 1) * 256].rearrange("p (j q) -> p j q", j=2),
                       KT[64 * e2:64 * e2 + 64, g, kslot * 128:(kslot + 1) * 128],
                       aT[64 * e2:64 * e2 + 64, 8 + 2 * g:8 + 2 * g + 2, qc:qc + 128], True, True,
                       [("KT", g, kslot), ("aT", 8 + 2 * g), ("aT", 9 + 2 * g)], su)
            pms = []
            for kbi, kslot in enumerate(kslots):
                pi = rot("pt", 2)
                ptile = aT[:, 16 + pi, 0:512]
                ACT(ptile.rearrange("p (e q) -> p e q", e=2), pst[:, bi0:bi0 + 2, kbi * 256:(kbi + 1) * 256], AF.Exp,
                    su, [("aT", 16 + pi)], scale=0.125)
                mi = 2 if (kbi == 0 and ti == 0 and qb == 0) else kbi
                mpi = rot("pm", 4)
                pmtile = aT[:, 18 + mpi, 0:512]
                TT(pmtile, ptile, msk[:, mi, :], ALU.mult, [("aT", 16 + pi), "msk"], [("aT", 18 + mpi)])
                pms.append((pmtile, ("aT", 18 + mpi), kslot))
            return pms

        def a_s2(job, pms):
            qb, g = job
            qc = qb * 128
            ou, ob = nb()
            du, db = nb()
            for kbi, (pmtile, pmu, kslot) in enumerate(pms):
                for e2 in range(2):
                    MM(ob[64 * e2:64 * e2 + 64, 0:256], Vb[:, kslot, g, :], pmtile[:, e2 * 256:(e2 + 1) * 256], kbi == 0, kbi == 1,
                       [("Vb", kslot), pmu], [ou])
                for e2 in range(2):
                    MM(db[64 * e2:64 * e2 + 64, 0:256], ones[:, 0:64], pmtile[:, e2 * 256:(e2 + 1) * 256], kbi == 0, kbi == 1,
                       ["ones", pmu], [du])
            ri = rot("rd", 2)
            r = rd[ri]
            esap = bass.AP(esg, 2 * g, [[8, 128], [1, 2], [0, 128]])
            r3 = r[:, 0:256].rearrange("p (j q) -> p j q", j=2)
            TT(r3, db[:, 0:256].rearrange("p (j q) -> p j q", j=2), esap, ALU.add, [du, "esg"], [("rd", ri)])
            ACT(r[:, 0:256], r[:, 0:256], AF.Ln, [("rd", ri)], [("rd", ri)])
            ACT(r[:, 0:256], r[:, 0:256], AF.Exp, [("rd", ri)], [("rd", ri)], scale=-1.0)
            TT(yT[:, 2 * g:2 * g + 2, q0 + qc:q0 + qc + 128], ob[:, 0:256].rearrange("p (j q) -> p j q", j=2), r3, ALU.mult,
               [ou, ("rd", ri)], [("yT", 2 * g), ("yT", 2 * g + 1)])

        pipeline([(qb, g) for qb in range(nqb) for g in range(ng)], a_s1, a_s2)

    def load_x(ti):
        T = TILES[ti]
        h, hname = hb[ti % 2], "h%d" % (ti % 2)
        for k in range(8):
            DMA("pool" if ti > 0 else "sp", h[:, k, 0:T["n0"]], xT.ap()[:, k, T["x0"]:T["x0"] + T["n0"]], [], [(hname, k)])

    def hb_of(ti):
        return hb[ti % 2], "h%d" % (ti % 2)

    def out_dma(ti, from_mix=False):
        h, hname = hb_of(ti)
        q0 = TILES[ti]["q0"]
        if from_mix:
            for k in range(8):
                DMA("sp", oT.ap()[:, k, ti * 512:(ti + 1) * 512], mix[:, k, q0:q0 + 512], [("mix", k)], [("out", ti, k)])
                outs.append(("out", ti, k))
            return
        else:
            DMA("sp", oT.ap()[:, :, ti * 512:(ti + 1) * 512], h[:, :, q0:q0 + 512], [(hname, k) for k in range(8)], [("out", ti)])
        outs.append(("out", ti))

    outs = []
    if dbg == "attn":
        load_x(0)
        T = TILES[0]
        DMA("pool", aT[:, :, 0:512], wst.ap()[:, 0:FC * 512].rearrange("p (c n) -> p c n", c=FC), [], [("aT", f) for f in range(FC)])
        DMA("pool", KT[:, :, :], wst.ap()[:, 20000:20000 + 2560].rearrange("p (c n) -> p c n", c=4), [], [("KT", g, sl_) for g in range(4) for sl_ in range(5)])
        DMA("pool", Vb[:, :, :, :], wst.ap()[:, 30000:30000 + 1280].rearrange("p (a b c) -> p a b c", a=5, b=4), [], [("Vb", i) for i in range(5)])
        attn_main(hb[0], "h0", T, 0, aT, "aT")
        for k in range(8):
            S.op("dve", lambda e, k=k: e.tensor_copy(out=hb[0][:, k, 0:512], in_=yT[:, k, 130:642]), reads=[("yT", k), ("h0", k)], writes=[("h0", k)])
        DMA("sp", oT.ap()[:, :, 0:512], hb[0][:, :, 0:512], [("h0", k) for k in range(8)], [("out", 0)])
        S.op("sp", lambda e: e.nop(), reads=[("out", 0)])
        S.emit()
        return nc
    if dbg == "L0":
        for ti, T in enumerate(TILES):
            h, hname = hb_of(ti)
            load_x(ti)
            conv_pre(h, hname, T)
            conv_main(h, hname, T)
            ffn_pre(h, hname, T["subs"], P_F_PRE0, xn, "xn", 0)
            ffn_main(h, hname, T["subs"], 0, P_F_POST0, xn, "xn", 0)
            out_dma(ti)
    else:
        def make_prefetch(tn):
            Tn = TILES[tn]
            hn, hnname = hb_of(tn)
            (so, n), = Tn["subs"]
            st = {}

            def hook(m):
                if m == ("gu", 0):
                    load_x(tn)
                elif m == ("gu", 7):
                    for k in range(8):
                        ACT(sq[:, k, so:so + n], hn[:, k, so:so + n], AF.Square, [(hnname, k)], [("sq", k)])
                elif m == ("gu", 9):
                    ri = rot("rt", 2)
                    st["r"] = (rtb[ri], ("rt", ri))
                    rstats(so, n, *st["r"])
                elif m == ("dn", 0):
                    norm_apply(hn, hnname, so, n, P_A_PRE, xn, "xn", 0, *st["r"])
            return hook

        pend = []

        def flush():
            while pend:
                pend.pop(0)()

        def gu1(extra=None):
            def hook(m):
                if m == ("gu", 5):
                    flush()
                if extra is not None:
                    extra(m)
            return hook

        def stage_C(tn):
            Tn = TILES[tn]
            hn, hnname = hb_of(tn)
            conv_main(hn, hnname, Tn, mid=flush, tail_to=pend)
            pend.append(lambda: ffn_pre(hn, hnname, Tn["subs"], P_F_PRE0, xn, "xn", 0))

        def stage_F0(tn, extra=None):
            Tn = TILES[tn]
            hn, hnname = hb_of(tn)
            flush_needed = [p for p in pend]
            ffn_main(hn, hnname, Tn["subs"], 0, P_F_POST0, xn, "xn", 0, hook=gu1(extra), tail_to=pend)

        def stage_A(ti, xkvb, xkvn, mid_extra=None, pre_q=None):
            T = TILES[ti]
            h, hname = hb_of(ti)

            def mid():
                flush()
                if mid_extra is not None:
                    mid_extra()
            attn_main(h, hname, T, ti, xkvb, xkvn, mid=mid, pre_q=pre_q, tail_to=pend)
            pend.append(lambda: ffn_pre(h, hname, [(T["q0"], 512)], P_F_PRE1, xnB, "xnB", T["q0"]))

        def stage_F1(ti, pre_out=None, defer=False):
            T = TILES[ti]
            h, hname = hb_of(ti)

            def fin():
                if pre_out is not None:
                    pre_out()
                out_dma(ti, from_mix=True)
            if defer:
                ffn_main(h, hname, [(T["q0"], 512)], 1, P_F_POST1, xnB, "xnB", T["q0"], to_mix=True, hook=gu1(), tail_to=pend)
                pend.append(fin)
            else:
                ffn_main(h, hname, [(T["q0"], 512)], 1, P_F_POST1, xnB, "xnB", T["q0"], to_mix=True, hook=gu1())
                fin()

        h, hname = hb_of(0)
        load_x(0)
        conv_pre(h, hname, TILES[0])
        stage_C(0)
        flush()
        stage_F0(0, extra=make_prefetch(1))
        st2 = {}

        def c2_0():
            hh, hhn = hb_of(0)
            r_ = attn_pre(hh, hhn, TILES[0], 0, aT, "aT")
            attn_pre_q(hh, hhn, TILES[0], r_)
        pend.append(c2_0)
        for ti in range(NT):
            T = TILES[ti]
            h, hname = hb_of(ti)
            nx = ti + 1 < NT
            if ti % 2 == 1:
                rr = st2["rr"]
                xk = (yT, "yT")
                mid_extra = None
                if nx:
                    mid_extra = (lambda tn=ti + 1: conv_pre(*hb_of(tn), TILES[tn]))
                stage_A(ti, *xk, mid_extra=mid_extra, pre_q=(lambda h=h, hname=hname, T=T, rr=rr: attn_pre_q(h, hname, T, rr)))
                if nx:
                    stage_C(ti + 1)
                else:
                    flush()
                stage_F1(ti)
                if nx:
                    stage_F0(ti + 1, extra=(make_prefetch(ti + 2) if ti + 2 < NT else None))
                    if ti + 1 < NT:
                        def c2(tn=ti + 1):
                            hh, hhn = hb_of(tn)
                            r_ = attn_pre(hh, hhn, TILES[tn], tn, aT, "aT")
                            attn_pre_q(hh, hhn, TILES[tn], r_)
                        pend.append(c2)
            else:
                if nx:
                    stage_C(ti + 1)
                else:
                    flush()
                stage_A(ti, aT, "aT")
                if nx:
                    stage_F0(ti + 1)

                    def c2kv(tn=ti + 1):
                        st2["rr"] = attn_pre(*hb_of(tn), TILES[tn], tn, yT, "yT")
                    pend.append(c2kv)
                else:
                    flush()
                pre_out = (lambda tn=ti + 2: load_x(tn)) if ti + 2 < NT else None
                stage_F1(ti, pre_out=pre_out, defer=nx)
        flush()
    S.op("sp", lambda e: e.nop(), reads=outs)
    S.emit()
    return nc


def _fm(vec):
    return np.ascontiguousarray(np.asarray(vec, np.float32).reshape(8, 128).T)


def _wrows(w):
    K, N = w.shape
    return np.asarray(w, np.float32).reshape(K // 128, 128, N).transpose(1, 0, 2)


def _build_wstream(a_w_in, a_w_out, gu, dn, w_kv, b_w_q, b_w_o):
    ws = np.empty((128, TOTC), np.float32)

    def put(name, arr):
        off, ncol = UOFF[name]
        ws[:, off:off + ncol] = arr.reshape(128, ncol)

    win = _wrows(a_w_in[0])
    for f in range(8):
        blk = np.stack([win[:, :, j * 1024 + f * 128:j * 1024 + (f + 1) * 128] for j in range(3)], axis=2)
        put("in%d" % f, blk)

    def outlike(w, names):
        wr = _wrows(w)
        for j in range(2):
            blk = np.stack([wr[:, :, (4 * j + ml) * 128:(4 * j + ml + 1) * 128] for ml in range(4)], axis=1)
            put(names[j], blk)

    outlike(a_w_out[0], ["out0", "out1"])
    outlike(b_w_q[0], ["q0", "q1"])
    outlike(b_w_o[0], ["o0", "o1"])
    for l in range(2):
        g = _wrows(gu[l])
        for j in range(11):
            parts = []
            for fl in range(2):
                f = 2 * j + fl
                for gi in range(2):
                    parts.append(g[:, :, gi * DFF + f * 128:gi * DFF + (f + 1) * 128])
            put("gu%d_%d" % (l, j), np.stack(parts, axis=1))
        d = _wrows(dn[l])
        for m in range(8):
            put("dn%d_%d" % (l, m), np.ascontiguousarray(d[:, :, m * 128:(m + 1) * 128]))
    kv = _wrows(w_kv)
    kparts = []
    for g4 in range(4):
        hk = kv[:, :, g4 * 64:(g4 + 1) * 64]
        kparts.append(np.concatenate([hk, hk], axis=2))
    put("k", np.stack(kparts, axis=1))
    put("v", np.ascontiguousarray(kv[:, :, 256:512]))
    return ws


_CACHE = {}


def kernel(x, a_pre_norm, a_w_in, a_conv_w, a_w_out, a_post_norm,
           ffn_pre_norm, ffn_w_gate_up, ffn_w_down, ffn_post_norm,
           kv_norm, w_kv, b_pre_norm, b_w_q, b_sinks, b_w_o, b_post_norm, _dbg=None):
    x = np.asarray(x, np.float32)
    prm = np.zeros((128, NPRM), np.float32)
    for base, vec in ((P_A_PRE, a_pre_norm[0]), (P_A_POST, a_post_norm[0]), (P_F_PRE0, ffn_pre_norm[0]),
                      (P_F_POST0, ffn_post_norm[0]), (P_KVN, kv_norm), (P_B_PRE, b_pre_norm[0]), (P_B_POST, b_post_norm[0]),
                      (P_F_PRE1, ffn_pre_norm[1]), (P_F_POST1, ffn_post_norm[1])):
        prm[:, base:base + 8] = _fm(vec)
    for j in range(3):
        prm[:, P_CW + 8 * j:P_CW + 8 * j + 8] = _fm(np.asarray(a_conv_w)[0, j])
    prm[:, P_SINK:P_SINK + 16] = np.asarray(b_sinks, np.float32)[0][None, :]
    wst = _build_wstream(np.asarray(a_w_in), np.asarray(a_w_out), np.asarray(ffn_w_gate_up), np.asarray(ffn_w_down),
                         np.asarray(w_kv), np.asarray(b_w_q), np.asarray(b_w_o))
    rm = np.zeros((128, 128), np.float32)
    for hh in range(2):
        for d in range(8):
            rm[hh * 64 + d + 8, hh * 64 + d] = -1.0
            rm[hh * 64 + d, hh * 64 + d + 8] = 1.0
    kk = np.arange(128)[:, None]
    qq = np.arange(128)[None, :]
    mprev = np.tile((kk > qq).astype(np.float32), (1, 4))
    mcur = np.tile((kk <= qq).astype(np.float32), (1, 4))
    inv_freq = (np.float32(ROPE_THETA) ** (-np.arange(0, 16, 2, dtype=np.float32) / np.float32(16))).astype(np.float32)
    in_maps = []
    for c in range(NCORES):
        b, half = c // 2, c % 2
        s0 = half * TOK
        xs = np.zeros((XT, D), np.float32)
        lo = s0 - 130
        src_lo = max(lo, 0)
        xs[src_lo - lo:] = x[b, src_lo:s0 + TOK]
        xTc = np.ascontiguousarray(xs.reshape(XT, 8, 128).transpose(2, 1, 0))
        pos = (s0 - 128 + np.arange(TOK + 128)).astype(np.float32)
        ang = pos[:, None] * inv_freq[None, :]
        cosv, sinv = np.cos(ang).astype(np.float32), np.sin(ang).astype(np.float32)
        rope = np.zeros((128, 2, TOK + 128), np.float32)
        rope[:, 0, :] = 1.0
        for hh in range(2):
            for d in range(16):
                rope[hh * 64 + d, 0, :] = cosv[:, d % 8]
                rope[hh * 64 + d, 1, :] = sinv[:, d % 8]
        msk = np.stack([mprev, mcur, mprev if half == 1 else np.zeros_like(mprev)], axis=1)
        in_maps.append({"xT": xTc, "wst": wst, "prm": prm, "rope": rope, "msk": np.ascontiguousarray(msk), "rm": rm})
    key = _dbg
    if key not in _CACHE:
        _CACHE[key] = build_program(_dbg)
    nc = _CACHE[key]
    res = run_bass_kernel_spmd(nc, in_maps, core_ids=list(range(NCORES)))
    out = np.empty((BATCH, SEQ, D), np.float32)
    for c in range(NCORES):
        b, half = c // 2, c % 2
        o = res.results[c]["oT"]
        out[b, half * TOK:(half + 1) * TOK, :] = o.transpose(2, 1, 0).reshape(TOK, D)
    return out
```

```python
import math
import numpy as np
import concourse.bass as bass
import concourse.mybir as mybir
from concourse.bass_utils import run_bass_kernel_spmd

F32 = mybir.dt.float32
BF16 = mybir.dt.bfloat16
ALU = mybir.AluOpType
AF = mybir.ActivationFunctionType

D = 1024
DFF = 2816
FC = DFF // 128
SEQ = 4096
BATCH = 4
NCORES = 8
TOK = 2048
XT = TOK + 130
W = 642
EPS = 1e-6
ROPE_THETA = 500000.0
NS = 5
ATTN_DBG = "11"
SLOT = 4096

TILES = [
    dict(n0=642, subs=[(0, 321), (321, 321)], kv0=2, kvp=[(0, 320), (320, 320)], nkb=5, q0=130, x0=0),
    dict(n0=512, subs=[(0, 512)], kv0=0, kvp=[(0, 512)], nkb=4, q0=0, x0=642),
    dict(n0=512, subs=[(0, 512)], kv0=0, kvp=[(0, 512)], nkb=4, q0=0, x0=1154),
    dict(n0=512, subs=[(0, 512)], kv0=0, kvp=[(0, 512)], nkb=4, q0=0, x0=1666),
]

P_A_PRE, P_A_POST, P_F_PRE0, P_F_POST0, P_KVN, P_B_PRE, P_B_POST, P_F_PRE1, P_F_POST1 = [8 * i for i in range(9)]
P_CW = 72
P_SINK = 96
NPRM = 112

UNITS = ([("in%d" % f, 3072) for f in range(8)] + [("out%d" % j, 4096) for j in range(2)] +
         [("gu0_%d" % j, 4096) for j in range(11)] + [("dn0_%d" % m, 2816) for m in range(8)] +
         [("k", 4096), ("v", 2048)] + [("q%d" % j, 4096) for j in range(2)] + [("o%d" % j, 4096) for j in range(2)] +
         [("gu1_%d" % j, 4096) for j in range(11)] + [("dn1_%d" % m, 2816) for m in range(8)])
UOFF = {}
_o = 0
for _n, _c in UNITS:
    UOFF[_n] = (_o, _c)
    _o += _c
TOTC = _o


class _Unit:
    __slots__ = ("lw", "rd")

    def __init__(self):
        self.lw = None
        self.rd = []


class _Op:
    __slots__ = ("eng", "fn", "deps", "needed", "sem", "inc", "sig", "clock", "eidx", "isdma")


class Sched:
    ENGS = ("pe", "act", "dve", "pool", "sp")

    def __init__(self, nc, ndma_sems=16):
        self.nc = nc
        self.eng = {"pe": nc.tensor, "act": nc.scalar, "dve": nc.vector, "pool": nc.gpsimd, "sp": nc.sync}
        self.ops = []
        self.ecount = {e: 0 for e in self.ENGS}
        self.esem = {e: nc.alloc_semaphore("s_" + e) for e in self.ENGS}
        self.dpool, self.dpos, self.dlast = {}, {}, {}
        for q in ("sp", "pool"):
            self.dpool[q] = [nc.alloc_semaphore("d_%s%d" % (q, i)) for i in range(ndma_sems)]
            self.dpos[q] = 0
            self.dlast[q] = [None] * ndma_sems
        self.units = {}

    def _U(self, xs, out):
        for x in xs:
            if isinstance(x, list):
                self._U(x, out)
            else:
                u = self.units.get(x)
                if u is None:
                    u = _Unit()
                    self.units[x] = u
                out.append(u)
        return out

    def op(self, eng, fn, reads=(), writes=(), dma=0):
        reads = self._U(reads, [])
        writes = self._U(writes, [])
        o = _Op()
        o.eng, o.fn, o.needed, o.isdma = eng, fn, False, dma > 0
        o.inc = 16 * dma if dma else 1
        o.sig = o.clock = None
        o.eidx = self.ecount[eng]
        self.ecount[eng] += 1
        deps = []
        for u in reads:
            if u.lw is not None:
                deps.append(u.lw)
        for u in writes:
            if u.lw is not None:
                deps.append(u.lw)
            deps.extend(u.rd)
        if dma:
            i = self.dpos[eng]
            self.dpos[eng] = (i + 1) % len(self.dpool[eng])
            o.sem = self.dpool[eng][i]
            if self.dlast[eng][i] is not None:
                deps.append(self.dlast[eng][i])
            self.dlast[eng][i] = o
            o.needed = True
        else:
            o.sem = self.esem[eng]
        fd, seen = [], set()
        for d in deps:
            if d is o or id(d) in seen:
                continue
            seen.add(id(d))
            if (not d.isdma) and d.eng == eng:
                if eng in ("pe", "sp"):
                    continue
                if o.eidx - d.eidx > 4:
                    continue
            fd.append(d)
            d.needed = True
        o.deps = fd
        ws = set()
        for u in writes:
            u.lw = o
            u.rd = []
            ws.add(id(u))
        for u in reads:
            if id(u) in ws:
                continue
            if not o.isdma:
                u.rd = [r for r in u.rd if r.isdma or r.eng != eng]
            u.rd.append(o)
        self.ops.append(o)
        return o

    def emit(self):
        clock = {e: {} for e in self.ENGS}
        semval = {}
        for o in self.ops:
            ck = clock[o.eng]
            waits = {}
            for d in o.deps:
                s, v = d.sig
                if ck.get(id(s), (None, 0))[1] >= v:
                    continue
                if id(s) not in waits or waits[id(s)][1] < v:
                    waits[id(s)] = (s, v)
            for d in o.deps:
                for k, sv in d.clock.items():
                    if ck.get(k, (None, 0))[1] < sv[1]:
                        ck[k] = sv
            wl = list(waits.values())
            e = self.eng[o.eng]
            for (s, v) in wl[1:]:
                e.wait_ge(s, v)
            ins = o.fn(e)
            if wl:
                ins._wait_ge(wl[0][0], wl[0][1])
            if o.needed:
                v = semval.get(id(o.sem), 0) + o.inc
                semval[id(o.sem)] = v
                o.sig = (o.sem, v)
                ins.then_inc(o.sem, o.inc)
                c = dict(ck)
                c[id(o.sem)] = (o.sem, v)
                o.clock = c
            o.fn = None
            o.deps = None


def build_program(dbg=None):
    nc = bass.Bass("TRN2", target_bir_lowering=False)
    xT = nc.dram_tensor("xT", [128, 8, XT], F32, kind="ExternalInput")
    wst = nc.dram_tensor("wst", [128, TOTC], F32, kind="ExternalInput")
    prm_d = nc.dram_tensor("prm", [128, NPRM], F32, kind="ExternalInput")
    rope_d = nc.dram_tensor("rope", [128, 2, TOK + 128], F32, kind="ExternalInput")
    msk_d = nc.dram_tensor("msk", [128, 3, 512], F32, kind="ExternalInput")
    rm_d = nc.dram_tensor("rm", [128, 128], F32, kind="ExternalInput")
    oT = nc.dram_tensor("oT", [128, 8, TOK], F32, kind="ExternalOutput")

    S = Sched(nc)
    A = nc.alloc_sbuf_tensor
    hb = [A("h0", [128, 8, W], F32), A("h1", [128, 8, W], F32)]
    xn = A("xn", [128, 8, W], BF16)
    xnB = A("xnB", [128, 8, 512], BF16)
    sq = A("sq", [128, 8, W], BF16)
    yT = A("yT", [128, 8, W], BF16)
    mix = A("mix", [128, 8, W], F32)
    aT = A("aT", [128, FC, W], BF16)
    rtb = [A("rt0", [128, W], F32), A("rt1", [128, W], F32)]
    cu = A("cu", [128, W + 2], F32)
    usb = [A("usb%d" % i, [128, 512], F32) for i in range(2)]
    tmpf = [A("tmpf%d" % i, [128, 512], F32) for i in range(3)]
    sgb = [A("sg%d" % i, [128, 512], BF16) for i in range(2)]
    carry = A("carry", [128, 8, 2], F32)
    ring = A("ring", [128, NS, SLOT], BF16)
    kraw = [A("kraw%d" % i, [128, 512], BF16) for i in range(2)]
    KT = A("KT", [128, 4, 640], BF16)
    Vb = A("Vb", [128, 5, 4, 64], BF16)
    esg = A("esg", [128, 8], F32)
    ropet = A("ropet", [128, 2, 640], F32)
    msk = A("mskb", [128, 3, 512], BF16)
    rm = A("rmb", [128, 128], BF16)
    ones = A("ones", [128, 128], BF16)
    prm = A("prmb", [128, NPRM], F32)
    es = A("es", [128, 16], F32)
    epsb = A("epsb", [128, 1], F32)
    rd = [A("rd%d" % i, [128, 256], F32) for i in range(2)]
    pst = nc.alloc_psum_tensor("pst", [128, 8, 512], F32)
    ps = [pst[:, i, :] for i in range(8)]

    cnt = {"ps": 0, "w": 0, "rt": 0, "usb": 0, "tmp": 0, "sg": 0, "kraw": 0, "rd": 0, "pt": 0, "pm": 0}

    def rot(name, n):
        i = cnt[name]
        cnt[name] = (i + 1) % n
        return i

    def nb():
        i = rot("ps", 8)
        return ("ps", i), ps[i]

    def nb2():
        i = cnt["ps"]
        i = (i + (i % 2)) % 8
        cnt["ps"] = (i + 2) % 8
        return [("ps", i), ("ps", i + 1)], i

    def MM(out, lhsT, rhs, st, sp, reads, writes):
        S.op("pe", lambda e: e.matmul(out, lhsT, rhs, start=st, stop=sp), reads, writes)

    def ACT(out, in_, func, reads, writes, **kw):
        S.op("act", lambda e: e.activation(out=out, in_=in_, func=func, **kw), reads, writes)

    def ACOPY(out, in_, reads, writes):
        S.op("act", lambda e: e.copy(out=out, in_=in_), reads, writes)

    def TT(out, in0, in1, op, reads, writes):
        S.op("dve", lambda e: e.tensor_tensor(out=out, in0=in0, in1=in1, op=op), reads, writes)

    def STT(out, in0, scalar, in1, op0, op1, reads, writes):
        S.op("dve", lambda e: e.scalar_tensor_tensor(out=out, in0=in0, scalar=scalar, in1=in1, op0=op0, op1=op1), reads, writes)

    def TSM(out, in0, scalar, reads, writes):
        S.op("dve", lambda e: e.tensor_scalar(out=out, in0=in0, scalar1=scalar, scalar2=None, op0=ALU.mult), reads, writes)

    def RECIP(out, in_, reads, writes):
        S.op("dve", lambda e: e.reciprocal(out=out, in_=in_), reads, writes)

    def DMA(q, out, in_, reads, writes):
        S.op(q, lambda e: e.dma_start(out=out, in_=in_), reads, writes, dma=1)

    DMA("sp", prm[:, :], prm_d.ap(), [], ["prm"])
    DMA("pool", msk[:, :, :], msk_d.ap(), [], ["msk"])
    DMA("pool", rm[:, :], rm_d.ap(), [], ["rm"])
    S.op("dve", lambda e: e.memset(ones[:, :], 1.0), writes=["ones"])
    S.op("dve", lambda e: e.memset(epsb[:, :], EPS), writes=["epsb"])
    S.op("dve", lambda e: e.memset(carry[:, :, :], 0.0), writes=["carry"])
    ACT(es[:, :], prm[:, P_SINK:P_SINK + 16], AF.Exp, ["prm"], ["es"])
    for e2 in range(2):
        ACOPY(esg[64 * e2:64 * e2 + 64, :].rearrange("p (g j) -> p g j", g=4),
              es[64 * e2:64 * e2 + 64, :].rearrange("p (g j e) -> p g j e", g=4, j=2, e=2)[:, :, :, e2], ["es"], ["esg"])

    wstate = {"issued": 0}
    U_CONV = ["in%d" % f for f in range(8)] + ["out0", "out1"]
    U_FFN = [["gu%d_%d" % (l, j) for j in range(11)] + ["dn%d_%d" % (l, m) for m in range(8)] for l in range(2)]
    U_ATT = ["v", "k", "q0", "q1", "o0", "o1"]
    NT = len(TILES)
    if dbg == "L0":
        seq = (U_CONV + U_FFN[0]) * NT
    else:
        seq = U_CONV + U_FFN[0]
        for i in range(NT):
            nx_ = i + 1 < NT
            if i % 2 == 1:
                seq = seq + U_ATT + (U_CONV if nx_ else []) + U_FFN[1] + (U_FFN[0] if nx_ else [])
            else:
                seq = seq + (U_CONV if nx_ else []) + U_ATT + (U_FFN[0] if nx_ else []) + U_FFN[1]

    def wnext(name):
        j = cnt["w"]
        cnt["w"] = j + 1
        assert seq[j] == name, (seq[j], name)
        while wstate["issued"] < min(len(seq), j + NS):
            i = wstate["issued"]
            off, ncol = UOFF[seq[i]]
            sl = i % NS
            hold = [("h0", k) for k in range(8)] if (1 <= i < NS and dbg != "attn") else []
            DMA("pool", ring[:, sl, 0:ncol], wst.ap()[:, off:off + ncol], hold, [("ring", sl)])
            wstate["issued"] = i + 1
        sl = j % NS
        return ("ring", sl), sl

    def gcol(base, k):
        return prm[:, base + k:base + k + 1]

    def pipeline(jobs, s1, s2):
        prev = None
        for j in jobs:
            st_ = s1(j)
            if prev is not None:
                s2(*prev)
            prev = (j, st_)
        if prev is not None:
            s2(*prev)

    def rstats(so, n, rbuf, runit):
        pu, p = nb()
        for k in range(8):
            MM(p[:, 0:n], ones[:, :], sq[:, k, so:so + n], k == 0, k == 7, [("sq", k), "ones"], [pu])
        ACT(rbuf[:, so:so + n], p[:, 0:n], AF.Ln, [pu, "epsb"], [runit], bias=epsb[:, 0:1], scale=1.0 / D)
        ACT(rbuf[:, so:so + n], rbuf[:, so:so + n], AF.Exp, [runit], [runit], scale=-0.5)

    def norm_apply(h, hname, so, n, gb, ob, oname, ooff, rbuf, runit):
        for k in range(8):
            STT(ob[:, k, so - ooff:so - ooff + n], h[:, k, so:so + n], gcol(gb, k), rbuf[:, so:so + n], ALU.mult, ALU.mult,
                [(hname, k), "prm", runit], [(oname, k)])

    def prenorm(h, hname, subs, specs, late=()):
        ri = rot("rt", 2)
        rbuf, runit = rtb[ri], ("rt", ri)
        for (so, n) in subs:
            for k in range(8):
                ACT(sq[:, k, so:so + n], h[:, k, so:so + n], AF.Square, [(hname, k)], [("sq", k)])
            rstats(so, n, rbuf, runit)
            for (gb, ob, oname, ooff) in specs:
                norm_apply(h, hname, so, n, gb, ob, oname, ooff, rbuf, runit)
        for (gb, ob, oname, ooff, so, n) in late:
            norm_apply(h, hname, so, n, gb, ob, oname, ooff, rbuf, runit)

    def proj_postnorm(h, hname, subs, src, sname, nk, unames, per_unit, gbase, to_mix=False, hook=None, tail_to=None):
        wu = sl = None
        for m in range(8):
            if hook is not None:
                hook(("dn", m))
            if m % per_unit == 0:
                wu, sl = wnext(unames[m // per_unit])
            ml = m % per_unit
            for (so, n) in subs:
                pu, p = nb()
                for k in range(nk):
                    c0 = (ml * nk + k) * 128
                    MM(p[:, 0:n], ring[:, sl, c0:c0 + 128], src[:, k, so:so + n], k == 0, k == nk - 1, [wu, (sname, k)], [pu])
                ACOPY(mix[:, m, so:so + n], p[:, 0:n], [pu], [("mix", m)])
                ACT(sq[:, m, so:so + n], p[:, 0:n], AF.Square, [pu], [("sq", m)])
        def tail():
            ri = rot("rt", 2)
            rbuf, runit = rtb[ri], ("rt", ri)
            for (so, n) in subs:
                rstats(so, n, rbuf, runit)
                for k in range(8):
                    ti_ = rot("tmp", 3)
                    t = tmpf[ti_]
                    STT(t[:, 0:n], mix[:, k, so:so + n], gcol(gbase, k), rbuf[:, so:so + n], ALU.mult, ALU.mult,
                        [("mix", k), "prm", runit], [("tmp", ti_)])
                    if to_mix:
                        TT(mix[:, k, so:so + n], h[:, k, so:so + n], t[:, 0:n], ALU.add, [("tmp", ti_), (hname, k), ("mix", k)], [("mix", k)])
                    else:
                        TT(h[:, k, so:so + n], h[:, k, so:so + n], t[:, 0:n], ALU.add, [("tmp", ti_), (hname, k)], [(hname, k)])

        if tail_to is None:
            tail()
        else:
            tail_to.append(tail)

    def ffn_pre(h, hname, subs, g_pre, xb, xname, xoff):
        prenorm(h, hname, subs, [(g_pre, xb, xname, xoff)])

    def ffn_main(h, hname, subs, layer, g_post, xb, xname, xoff, to_mix=False, hook=None, tail_to=None):
        for j in range(11):
            if hook is not None:
                hook(("gu", j))
            wu, sl = wnext("gu%d_%d" % (layer, j))
            for fl in range(2):
                f = 2 * j + fl
                for (so, n) in subs:
                    pgu, pg = nb()
                    puu, pup = nb()
                    for gi, (pp, ppu) in enumerate(((pg, pgu), (pup, puu))):
                        for k in range(8):
                            c0 = ((fl * 2 + gi) * 8 + k) * 128
                            MM(pp[:, 0:n], ring[:, sl, c0:c0 + 128], xb[:, k, so - xoff:so - xoff + n], k == 0, k == 7, [wu, (xname, k)], [ppu])
                    si = rot("sg", 2)
                    sg = sgb[si]
                    ACT(sg[:, 0:n], pg[:, 0:n], AF.Silu, [pgu], [("sg", si)])
                    TT(aT[:, f, so:so + n], pup[:, 0:n], sg[:, 0:n], ALU.mult, [puu, ("sg", si)], [("aT", f)])
        proj_postnorm(h, hname, subs, aT, "aT", FC, ["dn%d_%d" % (layer, m) for m in range(8)], 1, g_post, to_mix=to_mix, hook=hook, tail_to=tail_to)

    def conv_pre(h, hname, T):
        prenorm(h, hname, T["subs"], [(P_A_PRE, xn, "xn", 0)])

    def conv_main(h, hname, T, mid=None, tail_to=None, early=None):
        subs = T["subs"]
        n0 = T["n0"]
        for f in range(8):
            if f == 1 and early is not None:
                early()
            if f == 4 and mid is not None:
                mid()
            wu, sl = wnext("in%d" % f)
            ACOPY(cu[:, 0:2], carry[:, f, :], ["carry"], ["cu"])
            for (so, n) in subs:
                banks = [nb() for _ in range(3)]
                for j, (pu, p) in enumerate(banks):
                    for k in range(8):
                        c0 = k * 384 + j * 128
                        MM(p[:, 0:n], ring[:, sl, c0:c0 + 128], xn[:, k, so:so + n], k == 0, k == 7, [wu, ("xn", k)], [pu])
                (pbu, pb), (pcu, pc), (puu, pup) = banks
                ui = rot("usb", 2)
                ub = usb[ui]
                ACOPY(ub[:, 0:n], pup[:, 0:n], [puu], [("usb", ui)])
                TT(cu[:, 2 + so:2 + so + n], pc[:, 0:n], ub[:, 0:n], ALU.mult, [pcu, ("usb", ui)], ["cu"])
                t0i = rot("tmp", 3)
                t1i = rot("tmp", 3)
                t0, t1 = tmpf[t0i], tmpf[t1i]
                TSM(t0[:, 0:n], cu[:, 2 + so:2 + so + n], gcol(P_CW + 16, f), ["cu", "prm"], [("tmp", t0i)])
                STT(t1[:, 0:n], cu[:, 1 + so:1 + so + n], gcol(P_CW + 8, f), t0[:, 0:n], ALU.mult, ALU.add,
                    ["cu", "prm", ("tmp", t0i)], [("tmp", t1i)])
                STT(t0[:, 0:n], cu[:, so:so + n], gcol(P_CW, f), t1[:, 0:n], ALU.mult, ALU.add,
                    ["cu", "prm", ("tmp", t1i)], [("tmp", t0i)])
                TT(yT[:, f, so:so + n], pb[:, 0:n], t0[:, 0:n], ALU.mult, [pbu, ("tmp", t0i)], [("yT", f)])
            ACOPY(carry[:, f, :], cu[:, n0:n0 + 2], ["cu"], ["carry"])
        proj_postnorm(h, hname, subs, yT, "yT", 8, ["out0", "out1"], 4, P_A_POST, tail_to=tail_to)

    def rope_evac(pu, p, n, tab0, dst, dunits):
        ki = rot("kraw", 2)
        kr = kraw[ki]
        ACOPY(kr[:, 0:n], p[:, 0:n], [pu], [("kraw", ki)])
        p2u, p2 = nb()
        MM(p2[:, 0:n], rm[:, :], kr[:, 0:n], True, True, ["rm", ("kraw", ki)], [p2u])
        t0i = rot("tmp", 3)
        t1i = rot("tmp", 3)
        t0, t1 = tmpf[t0i], tmpf[t1i]
        TT(t0[:, 0:n], p[:, 0:n], ropet[:, 0, tab0:tab0 + n], ALU.mult, [pu, "ropet", ("kraw", ki)], [("tmp", t0i)])
        TT(t1[:, 0:n], p2[:, 0:n], ropet[:, 1, tab0:tab0 + n], ALU.mult, [p2u, "ropet"], [("tmp", t1i)])
        TT(dst, t0[:, 0:n], t1[:, 0:n], ALU.add, [("tmp", t0i), ("tmp", t1i)], dunits)

    def attn_pre(h, hname, T, ti, xkvb, xkvn):
        kv0, q0, nkb = T["kv0"], T["q0"], T["nkb"]
        kvl0 = T["x0"] + kv0 - 2
        nkv = nkb * 128
        DMA("sp", ropet[:, :, 0:nkv], rope_d.ap()[:, :, kvl0:kvl0 + nkv], [], ["ropet"])
        if ti > 0:
            ACOPY(KT[:, :, 0:128], KT[:, :, 512:640], [("KT", g, 4) for g in range(4)], [("KT", g, 0) for g in range(4)])
            ACOPY(Vb[:, 0, :, :], Vb[:, 4, :, :], [("Vb", 4)], [("Vb", 0)])
        ri = rot("rt", 2)
        rbuf, runit = rtb[ri], ("rt", ri)
        for (so, n) in T["subs"]:
            for k in range(8):
                ACT(sq[:, k, so:so + n], h[:, k, so:so + n], AF.Square, [(hname, k)], [("sq", k)])
            rstats(so, n, rbuf, runit)
            norm_apply(h, hname, so, n, P_KVN, xkvb, xkvn, 0, rbuf, runit)
        return rbuf, runit

    def attn_pre_q(h, hname, T, rr):
        q0 = T["q0"]
        norm_apply(h, hname, q0, 512, P_B_PRE, xnB, "xnB", q0, rr[0], rr[1])

    def attn_main(h, hname, T, ti, xkv, xkvn, mid=None, pre_q=None, tail_to=None, early=None):
        n0, kv0, q0, nkb = T["n0"], T["kv0"], T["q0"], T["nkb"]
        ks0 = 5 - nkb
        tq0 = q0 - kv0
        if dbg == "attn":
            return attn_core(T, ti, int(ATTN_DBG[0]), int(ATTN_DBG[1]))
        wu, sl = wnext("v")
        for bi in range(nkb):
            pu, p = nb()
            t0 = kv0 + bi * 128
            for k in range(8):
                MM(p[:, 0:256], xkv[:, k, t0:t0 + 128], ring[:, sl, k * 256:(k + 1) * 256], k == 0, k == 7, [wu, (xkvn, k)], [pu])
            vs = ks0 + bi
            ACOPY(Vb[:, vs, :, :], p[:, 0:256].rearrange("p (g d) -> p g d", g=4), [pu], [("Vb", vs)])
        if early is not None:
            early()
        wu, sl = wnext("k")
        kjobs = []
        for g in range(4):
            for (po, pn) in T["kvp"]:
                kjobs.append((g, po, pn, wu, sl))

        def k_s1(job):
            g, po, pn, wu_, sl_ = job
            pu, p = nb()
            for k in range(8):
                c0 = (g * 8 + k) * 128
                MM(p[:, 0:pn], ring[:, sl_, c0:c0 + 128], xkv[:, k, kv0 + po:kv0 + po + pn], k == 0, k == 7, [wu_, (xkvn, k)], [pu])
            return (pu, p)

        def k_s2(job, st_):
            g, po, pn, wu_, sl_ = job
            pu, p = st_
            c1 = ks0 * 128 + po
            rope_evac(pu, p, pn, po, KT[:, g, c1:c1 + pn], [("KT", g, s_) for s_ in range(5)])

        pipeline(kjobs, k_s1, k_s2)
        if pre_q is not None:
            pre_q()
        if mid is not None:
            mid()
        def q_s1(c):
            if c % 4 == 0:
                qst["w"] = wnext("q%d" % (c // 4))
            wu_, sl_ = qst["w"]
            pu, p = nb()
            for k in range(8):
                c0 = ((c % 4) * 8 + k) * 128
                MM(p[:, 0:512], ring[:, sl_, c0:c0 + 128], xnB[:, k, 0:512], k == 0, k == 7, [wu_, ("xnB", k)], [pu])
            return (pu, p)

        def q_s2(c, st_):
            pu, p = st_
            rope_evac(pu, p, 512, tq0, aT[:, 8 + c, 0:512], [("aT", 8 + c)])

        qst = {}
        pipeline(list(range(8)), q_s1, q_s2)
        attn_core(T, ti, 4, 4)
        proj_postnorm(h, hname, [(q0, 512)], yT, "yT", 8, ["o0", "o1"], 4, P_B_POST, tail_to=tail_to)

    def attn_core(T, ti, nqb, ng):
        q0 = T["q0"]
        def a_s1(job):
            qb, g = job
            qc = qb * 128
            su, bi0 = nb2()
            kslots = (qb, qb + 1)
            for kbi, kslot in enumerate(kslots):
                for e2 in range(2):
                    MM(pst[:, bi0 + e2, kbi * 256:(kbi + 1) * 256].rearrange("p (j q) -> p j q", j=2),
                       KT[64 * e2:64 * e2 + 64, g, kslot * 128:(kslot + 1) * 128],
                       aT[64 * e2:64 * e2 + 64, 8 + 2 * g:8 + 2 * g + 2, qc:qc + 128], True, True,
                       [("KT", g, kslot), ("aT", 8 + 2 * g), ("aT", 9 + 2 * g)], su)
            pms = []
            for kbi, kslot in enumerate(kslots):
                pi = rot("pt", 2)
                ptile = aT[:, 16 + pi, 0:512]
                ACT(ptile.rearrange("p (e q) -> p e q", e=2), pst[:, bi0:bi0 + 2, kbi * 256:(kbi + 1) * 256], AF.Exp,
                    su, [("aT", 16 + pi)], scale=0.125)
                mi = 2 if (kbi == 0 and ti == 0 and qb == 0) else kbi
                mpi = rot("pm", 4)
                pmtile = aT[:, 18 + mpi, 0:512]
                TT(pmtile, ptile, msk[:, mi, :], ALU.mult, [("aT", 16 + pi), "msk"], [("aT", 18 + mpi)])
                pms.append((pmtile, ("aT", 18 + mpi), kslot))
            return pms

        def a_s2(job, pms):
            qb, g = job
            qc = qb * 128
            ou, ob = nb()
            du, db = nb()
            for kbi, (pmtile, pmu, kslot) in enumerate(pms):
                for e2 in range(2):
                    MM(ob[64 * e2:64 * e2 + 64, 0:256], Vb[:, kslot, g, :], pmtile[:, e2 * 256:(e2 + 1) * 256], kbi == 0, kbi == 1,
                       [("Vb", kslot), pmu], [ou])
                for e2 in range(2):
                    MM(db[64 * e2:64 * e2 + 64, 0:256], ones[:, 0:64], pmtile[:, e2 * 256:(e2 + 1) * 256], kbi == 0, kbi == 1,
                       ["ones", pmu], [du])
            ri = rot("rd", 2)
            r = rd[ri]
            esap = bass.AP(esg, 2 * g, [[8, 128], [1, 2], [0, 128]])
            r3 = r[:, 0:256].rearrange("p (j q) -> p j q", j=2)
            TT(r3, db[:, 0:256].rearrange("p (j q) -> p j q", j=2), esap, ALU.add, [du, "esg"], [("rd", ri)])
            ACT(r[:, 0:256], r[:, 0:256], AF.Ln, [("rd", ri)], [("rd", ri)])
            ACT(r[:, 0:256], r[:, 0:256], AF.Exp, [("rd", ri)], [("rd", ri)], scale=-1.0)
            TT(yT[:, 2 * g:2 * g + 2, q0 + qc:q0 + qc + 128], ob[:, 0:256].rearrange("p (j q) -> p j q", j=2), r3, ALU.mult,
               [ou, ("rd", ri)], [("yT", 2 * g), ("yT", 2 * g + 1)])

        pipeline([(qb, g) for qb in range(nqb) for g in range(ng)], a_s1, a_s2)

    def load_x(ti):
        T = TILES[ti]
        h, hname = hb[ti % 2], "h%d" % (ti % 2)
        for k in range(8):
            DMA("pool" if ti > 0 else "sp", h[:, k, 0:T["n0"]], xT.ap()[:, k, T["x0"]:T["x0"] + T["n0"]], [], [(hname, k)])

    def hb_of(ti):
        return hb[ti % 2], "h%d" % (ti % 2)

    def out_dma(ti, from_mix=False):
        h, hname = hb_of(ti)
        q0 = TILES[ti]["q0"]
        if from_mix:
            for k in range(8):
                DMA("sp", oT.ap()[:, k, ti * 512:(ti + 1) * 512], mix[:, k, q0:q0 + 512], [("mix", k)], [("out", ti, k)])
                outs.append(("out", ti, k))
            return
        else:
            DMA("sp", oT.ap()[:, :, ti * 512:(ti + 1) * 512], h[:, :, q0:q0 + 512], [(hname, k) for k in range(8)], [("out", ti)])
        outs.append(("out", ti))

    outs = []
    if dbg == "attn":
        load_x(0)
        T = TILES[0]
        DMA("pool", aT[:, :, 0:512], wst.ap()[:, 0:FC * 512].rearrange("p (c n) -> p c n", c=FC), [], [("aT", f) for f in range(FC)])
        DMA("pool", KT[:, :, :], wst.ap()[:, 20000:20000 + 2560].rearrange("p (c n) -> p c n", c=4), [], [("KT", g, sl_) for g in range(4) for sl_ in range(5)])
        DMA("pool", Vb[:, :, :, :], wst.ap()[:, 30000:30000 + 1280].rearrange("p (a b c) -> p a b c", a=5, b=4), [], [("Vb", i) for i in range(5)])
        attn_main(hb[0], "h0", T, 0, aT, "aT")
        for k in range(8):
            S.op("dve", lambda e, k=k: e.tensor_copy(out=hb[0][:, k, 0:512], in_=yT[:, k, 130:642]), reads=[("yT", k), ("h0", k)], writes=[("h0", k)])
        DMA("sp", oT.ap()[:, :, 0:512], hb[0][:, :, 0:512], [("h0", k) for k in range(8)], [("out", 0)])
        S.op("sp", lambda e: e.nop(), reads=[("out", 0)])
        S.emit()
        return nc
    if dbg == "L0":
        for ti, T in enumerate(TILES):
            h, hname = hb_of(ti)
            load_x(ti)
            conv_pre(h, hname, T)
            conv_main(h, hname, T)
            ffn_pre(h, hname, T["subs"], P_F_PRE0, xn, "xn", 0)
            ffn_main(h, hname, T["subs"], 0, P_F_POST0, xn, "xn", 0)
            out_dma(ti)
    else:
        def make_prefetch(tn):
            Tn = TILES[tn]
            hn, hnname = hb_of(tn)
            (so, n), = Tn["subs"]
            st = {}

            def hook(m):
                if m == ("gu", 0):
                    load_x(tn)
                elif m == ("gu", 7):
                    for k in range(8):
                        ACT(sq[:, k, so:so + n], hn[:, k, so:so + n], AF.Square, [(hnname, k)], [("sq", k)])
                elif m == ("gu", 9):
                    ri = rot("rt", 2)
                    st["r"] = (rtb[ri], ("rt", ri))
                    rstats(so, n, *st["r"])
                elif m == ("dn", 0):
                    norm_apply(hn, hnname, so, n, P_A_PRE, xn, "xn", 0, *st["r"])
            return hook

        pend = []
        pend_t = []

        def flush_t():
            while pend_t:
                pend_t.pop(0)()

        def flush():
            flush_t()
            while pend:
                pend.pop(0)()

        def gu1(extra=None):
            def hook(m):
                if m == ("gu", 1):
                    flush_t()
                if m == ("gu", 5):
                    flush()
                if extra is not None:
                    extra(m)
            return hook

        def stage_C(tn):
            Tn = TILES[tn]
            hn, hnname = hb_of(tn)
            conv_main(hn, hnname, Tn, mid=flush, tail_to=pend_t, early=flush_t)
            pend.append(lambda: ffn_pre(hn, hnname, Tn["subs"], P_F_PRE0, xn, "xn", 0))

        def stage_F0(tn, extra=None):
            Tn = TILES[tn]
            hn, hnname = hb_of(tn)
            flush_needed = [p for p in pend]
            ffn_main(hn, hnname, Tn["subs"], 0, P_F_POST0, xn, "xn", 0, hook=gu1(extra), tail_to=pend_t)

        def stage_A(ti, xkvb, xkvn, mid_extra=None, pre_q=None):
            T = TILES[ti]
            h, hname = hb_of(ti)

            def mid():
                flush()
                if mid_extra is not None:
                    mid_extra()
            attn_main(h, hname, T, ti, xkvb, xkvn, mid=mid, pre_q=pre_q, tail_to=pend_t, early=flush_t)
            pend.append(lambda: ffn_pre(h, hname, [(T["q0"], 512)], P_F_PRE1, xnB, "xnB", T["q0"]))

        def stage_F1(ti, pre_out=None, defer=False):
            T = TILES[ti]
            h, hname = hb_of(ti)

            def fin():
                if pre_out is not None:
                    pre_out()
                out_dma(ti, from_mix=True)
            if defer:
                ffn_main(h, hname, [(T["q0"], 512)], 1, P_F_POST1, xnB, "xnB", T["q0"], to_mix=True, hook=gu1(), tail_to=pend_t)
                pend_t.append(fin)
            else:
                ffn_main(h, hname, [(T["q0"], 512)], 1, P_F_POST1, xnB, "xnB", T["q0"], to_mix=True, hook=gu1())
                fin()

        h, hname = hb_of(0)
        load_x(0)
        conv_pre(h, hname, TILES[0])
        stage_C(0)
        flush()
        stage_F0(0, extra=make_prefetch(1))
        st2 = {}

        def c2_0():
            hh, hhn = hb_of(0)
            r_ = attn_pre(hh, hhn, TILES[0], 0, aT, "aT")
            attn_pre_q(hh, hhn, TILES[0], r_)
        pend.append(c2_0)
        for ti in range(NT):
            T = TILES[ti]
            h, hname = hb_of(ti)
            nx = ti + 1 < NT
            if ti % 2 == 1:
                rr = st2["rr"]
                xk = (yT, "yT")
                mid_extra = None
                if nx:
                    mid_extra = (lambda tn=ti + 1: conv_pre(*hb_of(tn), TILES[tn]))
                stage_A(ti, *xk, mid_extra=mid_extra, pre_q=(lambda h=h, hname=hname, T=T, rr=rr: attn_pre_q(h, hname, T, rr)))
                if nx:
                    stage_C(ti + 1)
                else:
                    flush()
                stage_F1(ti)
                if nx:
                    stage_F0(ti + 1, extra=(make_prefetch(ti + 2) if ti + 2 < NT else None))
                    if ti + 1 < NT:
                        def c2(tn=ti + 1):
                            hh, hhn = hb_of(tn)
                            r_ = attn_pre(hh, hhn, TILES[tn], tn, aT, "aT")
                            attn_pre_q(hh, hhn, TILES[tn], r_)
                        pend.append(c2)
            else:
                if nx:
                    stage_C(ti + 1)
                else:
                    flush()
                stage_A(ti, aT, "aT")
                if nx:
                    stage_F0(ti + 1)

                    def c2kv(tn=ti + 1):
                        st2["rr"] = attn_pre(*hb_of(tn), TILES[tn], tn, yT, "yT")
                    pend.append(c2kv)
                else:
                    flush()
                pre_out = (lambda tn=ti + 2: load_x(tn)) if ti + 2 < NT else None
                stage_F1(ti, pre_out=pre_out, defer=nx)
        flush()
    S.op("sp", lambda e: e.nop(), reads=outs)
    S.emit()
    return nc


def _fm(vec):
    return np.ascontiguousarray(np.asarray(vec, np.float32).reshape(8, 128).T)


def _wrows(w):
    K, N = w.shape
    return np.asarray(w, np.float32).reshape(K // 128, 128, N).transpose(1, 0, 2)


def _build_wstream(a_w_in, a_w_out, gu, dn, w_kv, b_w_q, b_w_o):
    ws = np.empty((128, TOTC), np.float32)

    def put(name, arr):
        off, ncol = UOFF[name]
        ws[:, off:off + ncol] = arr.reshape(128, ncol)

    win = _wrows(a_w_in[0])
    for f in range(8):
        blk = np.stack([win[:, :, j * 1024 + f * 128:j * 1024 + (f + 1) * 128] for j in range(3)], axis=2)
        put("in%d" % f, blk)

    def outlike(w, names):
        wr = _wrows(w)
        for j in range(2):
            blk = np.stack([wr[:, :, (4 * j + ml) * 128:(4 * j + ml + 1) * 128] for ml in range(4)], axis=1)
            put(names[j], blk)

    outlike(a_w_out[0], ["out0", "out1"])
    outlike(b_w_q[0], ["q0", "q1"])
    outlike(b_w_o[0], ["o0", "o1"])
    for l in range(2):
        g = _wrows(gu[l])
        for j in range(11):
            parts = []
            for fl in range(2):
                f = 2 * j + fl
                for gi in range(2):
                    parts.append(g[:, :, gi * DFF + f * 128:gi * DFF + (f + 1) * 128])
            put("gu%d_%d" % (l, j), np.stack(parts, axis=1))
        d = _wrows(dn[l])
        for m in range(8):
            put("dn%d_%d" % (l, m), np.ascontiguousarray(d[:, :, m * 128:(m + 1) * 128]))
    kv = _wrows(w_kv)
    kparts = []
    for g4 in range(4):
        hk = kv[:, :, g4 * 64:(g4 + 1) * 64]
        kparts.append(np.concatenate([hk, hk], axis=2))
    put("k", np.stack(kparts, axis=1))
    put("v", np.ascontiguousarray(kv[:, :, 256:512]))
    return ws


_CACHE = {}


def kernel(x, a_pre_norm, a_w_in, a_conv_w, a_w_out, a_post_norm,
           ffn_pre_norm, ffn_w_gate_up, ffn_w_down, ffn_post_norm,
           kv_norm, w_kv, b_pre_norm, b_w_q, b_sinks, b_w_o, b_post_norm, _dbg=None):
    x = np.asarray(x, np.float32)
    prm = np.zeros((128, NPRM), np.float32)
    for base, vec in ((P_A_PRE, a_pre_norm[0]), (P_A_POST, a_post_norm[0]), (P_F_PRE0, ffn_pre_norm[0]),
                      (P_F_POST0, ffn_post_norm[0]), (P_KVN, kv_norm), (P_B_PRE, b_pre_norm[0]), (P_B_POST, b_post_norm[0]),
                      (P_F_PRE1, ffn_pre_norm[1]), (P_F_POST1, ffn_post_norm[1])):
        prm[:, base:base + 8] = _fm(vec)
    for j in range(3):
        prm[:, P_CW + 8 * j:P_CW + 8 * j + 8] = _fm(np.asarray(a_conv_w)[0, j])
    prm[:, P_SINK:P_SINK + 16] = np.asarray(b_sinks, np.float32)[0][None, :]
    wst = _build_wstream(np.asarray(a_w_in), np.asarray(a_w_out), np.asarray(ffn_w_gate_up), np.asarray(ffn_w_down),
                         np.asarray(w_kv), np.asarray(b_w_q), np.asarray(b_w_o))
    rm = np.zeros((128, 128), np.float32)
    for hh in range(2):
        for d in range(8):
            rm[hh * 64 + d + 8, hh * 64 + d] = -1.0
            rm[hh * 64 + d, hh * 64 + d + 8] = 1.0
    kk = np.arange(128)[:, None]
    qq = np.arange(128)[None, :]
    mprev = np.tile((kk > qq).astype(np.float32), (1, 4))
    mcur = np.tile((kk <= qq).astype(np.float32), (1, 4))
    inv_freq = (np.float32(ROPE_THETA) ** (-np.arange(0, 16, 2, dtype=np.float32) / np.float32(16))).astype(np.float32)
    in_maps = []
    for c in range(NCORES):
        b, half = c // 2, c % 2
        s0 = half * TOK
        xs = np.zeros((XT, D), np.float32)
        lo = s0 - 130
        src_lo = max(lo, 0)
        xs[src_lo - lo:] = x[b, src_lo:s0 + TOK]
        xTc = np.ascontiguousarray(xs.reshape(XT, 8, 128).transpose(2, 1, 0))
        pos = (s0 - 128 + np.arange(TOK + 128)).astype(np.float32)
        ang = pos[:, None] * inv_freq[None, :]
        cosv, sinv = np.cos(ang).astype(np.float32), np.sin(ang).astype(np.float32)
        rope = np.zeros((128, 2, TOK + 128), np.float32)
        rope[:, 0, :] = 1.0
        for hh in range(2):
            for d in range(16):
                rope[hh * 64 + d, 0, :] = cosv[:, d % 8]
                rope[hh * 64 + d, 1, :] = sinv[:, d % 8]
        msk = np.stack([mprev, mcur, mprev if half == 1 else np.zeros_like(mprev)], axis=1)
        in_maps.append({"xT": xTc, "wst": wst, "prm": prm, "rope": rope, "msk": np.ascontiguousarray(msk), "rm": rm})
    key = _dbg
    if key not in _CACHE:
        _CACHE[key] = build_program(_dbg)
    nc = _CACHE[key]
    res = run_bass_kernel_spmd(nc, in_maps, core_ids=list(range(NCORES)))
    out = np.empty((BATCH, SEQ, D), np.float32)
    for c in range(NCORES):
        b, half = c // 2, c % 2
        o = res.results[c]["oT"]
        out[b, half * TOK:(half + 1) * TOK, :] = o.transpose(2, 1, 0).reshape(TOK, D)
    return out
```

```python
import math
import numpy as np
import concourse.bass as bass
import concourse.mybir as mybir
from concourse.bass_utils import run_bass_kernel_spmd

F32 = mybir.dt.float32
BF16 = mybir.dt.bfloat16
ALU = mybir.AluOpType
AF = mybir.ActivationFunctionType

D = 1024
DFF = 2816
FC = DFF // 128
SEQ = 4096
BATCH = 4
NCORES = 8
TOK = 2048
XT = TOK + 130
W = 642
EPS = 1e-6
ROPE_THETA = 500000.0
NS = 5
ATTN_DBG = "11"
SLOT = 4096

TILES = [
    dict(n0=642, subs=[(0, 321), (321, 321)], kv0=2, kvp=[(0, 320), (320, 320)], nkb=5, q0=130, x0=0),
    dict(n0=512, subs=[(0, 512)], kv0=0, kvp=[(0, 512)], nkb=4, q0=0, x0=642),
    dict(n0=512, subs=[(0, 512)], kv0=0, kvp=[(0, 512)], nkb=4, q0=0, x0=1154),
    dict(n0=512, subs=[(0, 512)], kv0=0, kvp=[(0, 512)], nkb=4, q0=0, x0=1666),
]

P_A_PRE, P_A_POST, P_F_PRE0, P_F_POST0, P_KVN, P_B_PRE, P_B_POST, P_F_PRE1, P_F_POST1 = [8 * i for i in range(9)]
P_CW = 72
P_SINK = 96
NPRM = 112

UNITS = ([("in%d" % f, 3072) for f in range(8)] + [("out%d" % j, 4096) for j in range(2)] +
         [("gu0_%d" % j, 4096) for j in range(11)] + [("dn0_%d" % m, 2816) for m in range(8)] +
         [("k", 4096), ("v", 2048)] + [("q%d" % j, 4096) for j in range(2)] + [("o%d" % j, 4096) for j in range(2)] +
         [("gu1_%d" % j, 4096) for j in range(11)] + [("dn1_%d" % m, 2816) for m in range(8)])
UOFF = {}
_o = 0
for _n, _c in UNITS:
    UOFF[_n] = (_o, _c)
    _o += _c
TOTC = _o


class _Unit:
    __slots__ = ("lw", "rd")

    def __init__(self):
        self.lw = None
        self.rd = []


class _Op:
    __slots__ = ("eng", "fn", "deps", "needed", "sem", "inc", "sig", "clock", "eidx", "isdma")


class Sched:
    ENGS = ("pe", "act", "dve", "pool", "sp")

    def __init__(self, nc, ndma_sems=16):
        self.nc = nc
        self.eng = {"pe": nc.tensor, "act": nc.scalar, "dve": nc.vector, "pool": nc.gpsimd, "sp": nc.sync}
        self.ops = []
        self.ecount = {e: 0 for e in self.ENGS}
        self.esem = {e: nc.alloc_semaphore("s_" + e) for e in self.ENGS}
        self.dpool, self.dpos, self.dlast = {}, {}, {}
        for q in ("sp", "pool"):
            self.dpool[q] = [nc.alloc_semaphore("d_%s%d" % (q, i)) for i in range(ndma_sems)]
            self.dpos[q] = 0
            self.dlast[q] = [None] * ndma_sems
        self.units = {}

    def _U(self, xs, out):
        for x in xs:
            if isinstance(x, list):
                self._U(x, out)
            else:
                u = self.units.get(x)
                if u is None:
                    u = _Unit()
                    self.units[x] = u
                out.append(u)
        return out

    def op(self, eng, fn, reads=(), writes=(), dma=0):
        reads = self._U(reads, [])
        writes = self._U(writes, [])
        o = _Op()
        o.eng, o.fn, o.needed, o.isdma = eng, fn, False, dma > 0
        o.inc = 16 * dma if dma else 1
        o.sig = o.clock = None
        o.eidx = self.ecount[eng]
        self.ecount[eng] += 1
        deps = []
        for u in reads:
            if u.lw is not None:
                deps.append(u.lw)
        for u in writes:
            if u.lw is not None:
                deps.append(u.lw)
            deps.extend(u.rd)
        if dma:
            i = self.dpos[eng]
            self.dpos[eng] = (i + 1) % len(self.dpool[eng])
            o.sem = self.dpool[eng][i]
            if self.dlast[eng][i] is not None:
                deps.append(self.dlast[eng][i])
            self.dlast[eng][i] = o
            o.needed = True
        else:
            o.sem = self.esem[eng]
        fd, seen = [], set()
        for d in deps:
            if d is o or id(d) in seen:
                continue
            seen.add(id(d))
            if (not d.isdma) and d.eng == eng:
                if eng in ("pe", "sp"):
                    continue
                if o.eidx - d.eidx > 4:
                    continue
            fd.append(d)
            d.needed = True
        o.deps = fd
        ws = set()
        for u in writes:
            u.lw = o
            u.rd = []
            ws.add(id(u))
        for u in reads:
            if id(u) in ws:
                continue
            if not o.isdma:
                u.rd = [r for r in u.rd if r.isdma or r.eng != eng]
            u.rd.append(o)
        self.ops.append(o)
        return o

    def emit(self):
        clock = {e: {} for e in self.ENGS}
        semval = {}
        for o in self.ops:
            ck = clock[o.eng]
            waits = {}
            for d in o.deps:
                s, v = d.sig
                if ck.get(id(s), (None, 0))[1] >= v:
                    continue
                if id(s) not in waits or waits[id(s)][1] < v:
                    waits[id(s)] = (s, v)
            for d in o.deps:
                for k, sv in d.clock.items():
                    if ck.get(k, (None, 0))[1] < sv[1]:
                        ck[k] = sv
            wl = list(waits.values())
            e = self.eng[o.eng]
            for (s, v) in wl[1:]:
                e.wait_ge(s, v)
            ins = o.fn(e)
            if wl:
                ins._wait_ge(wl[0][0], wl[0][1])
            if o.needed:
                v = semval.get(id(o.sem), 0) + o.inc
                semval[id(o.sem)] = v
                o.sig = (o.sem, v)
                ins.then_inc(o.sem, o.inc)
                c = dict(ck)
                c[id(o.sem)] = (o.sem, v)
                o.clock = c
            o.fn = None
            o.deps = None


def build_program(dbg=None):
    nc = bass.Bass("TRN2", target_bir_lowering=False)
    xT = nc.dram_tensor("xT", [128, 8, XT], F32, kind="ExternalInput")
    wst = nc.dram_tensor("wst", [128, TOTC], F32, kind="ExternalInput")
    prm_d = nc.dram_tensor("prm", [128, NPRM], F32, kind="ExternalInput")
    rope_d = nc.dram_tensor("rope", [128, 2, TOK + 128], F32, kind="ExternalInput")
    msk_d = nc.dram_tensor("msk", [128, 3, 512], F32, kind="ExternalInput")
    rm_d = nc.dram_tensor("rm", [128, 128], F32, kind="ExternalInput")
    oT = nc.dram_tensor("oT", [128, 8, TOK], F32, kind="ExternalOutput")

    S = Sched(nc)
    A = nc.alloc_sbuf_tensor
    hb = [A("h0", [128, 8, W], F32), A("h1", [128, 8, W], F32)]
    xn = A("xn", [128, 8, W], BF16)
    xnB = A("xnB", [128, 8, 512], BF16)
    sq = A("sq", [128, 8, W], BF16)
    yT = A("yT", [128, 8, W], BF16)
    mix = A("mix", [128, 8, W], F32)
    aT = A("aT", [128, FC, W], BF16)
    rtb = [A("rt0", [128, W], F32), A("rt1", [128, W], F32)]
    cu = A("cu", [128, W + 2], F32)
    usb = [A("usb%d" % i, [128, 512], F32) for i in range(2)]
    tmpf = [A("tmpf%d" % i, [128, 512], F32) for i in range(3)]
    sgb = [A("sg%d" % i, [128, 512], BF16) for i in range(2)]
    carry = A("carry", [128, 8, 2], F32)
    ring = A("ring", [128, NS, SLOT], BF16)
    kraw = [A("kraw%d" % i, [128, 512], BF16) for i in range(2)]
    KT = A("KT", [128, 4, 640], BF16)
    Vb = A("Vb", [128, 5, 4, 64], BF16)
    esg = A("esg", [128, 8], F32)
    ropet = A("ropet", [128, 2, 640], F32)
    msk = A("mskb", [128, 3, 512], BF16)
    rm = A("rmb", [128, 128], BF16)
    ones = A("ones", [128, 128], BF16)
    prm = A("prmb", [128, NPRM], F32)
    es = A("es", [128, 16], F32)
    epsb = A("epsb", [128, 1], F32)
    rd = [A("rd%d" % i, [128, 256], F32) for i in range(2)]
    pst = nc.alloc_psum_tensor("pst", [128, 8, 512], F32)
    ps = [pst[:, i, :] for i in range(8)]

    cnt = {"ps": 0, "w": 0, "rt": 0, "usb": 0, "tmp": 0, "sg": 0, "kraw": 0, "rd": 0, "pt": 0, "pm": 0}

    def rot(name, n):
        i = cnt[name]
        cnt[name] = (i + 1) % n
        return i

    def nb():
        i = rot("ps", 8)
        return ("ps", i), ps[i]

    def nb2():
        i = cnt["ps"]
        i = (i + (i % 2)) % 8
        cnt["ps"] = (i + 2) % 8
        return [("ps", i), ("ps", i + 1)], i

    def MM(out, lhsT, rhs, st, sp, reads, writes):
        S.op("pe", lambda e: e.matmul(out, lhsT, rhs, start=st, stop=sp), reads, writes)

    def ACT(out, in_, func, reads, writes, **kw):
        S.op("act", lambda e: e.activation(out=out, in_=in_, func=func, **kw), reads, writes)

    def ACOPY(out, in_, reads, writes):
        S.op("act", lambda e: e.copy(out=out, in_=in_), reads, writes)

    def TT(out, in0, in1, op, reads, writes):
        S.op("dve", lambda e: e.tensor_tensor(out=out, in0=in0, in1=in1, op=op), reads, writes)

    def STT(out, in0, scalar, in1, op0, op1, reads, writes):
        S.op("dve", lambda e: e.scalar_tensor_tensor(out=out, in0=in0, scalar=scalar, in1=in1, op0=op0, op1=op1), reads, writes)

    def TSM(out, in0, scalar, reads, writes):
        S.op("dve", lambda e: e.tensor_scalar(out=out, in0=in0, scalar1=scalar, scalar2=None, op0=ALU.mult), reads, writes)

    def RECIP(out, in_, reads, writes):
        S.op("dve", lambda e: e.reciprocal(out=out, in_=in_), reads, writes)

    def DMA(q, out, in_, reads, writes):
        S.op(q, lambda e: e.dma_start(out=out, in_=in_), reads, writes, dma=1)

    DMA("sp", prm[:, :], prm_d.ap(), [], ["prm"])
    DMA("pool", msk[:, :, :], msk_d.ap(), [], ["msk"])
    DMA("pool", rm[:, :], rm_d.ap(), [], ["rm"])
    S.op("dve", lambda e: e.memset(ones[:, :], 1.0), writes=["ones"])
    S.op("dve", lambda e: e.memset(epsb[:, :], EPS), writes=["epsb"])
    S.op("dve", lambda e: e.memset(carry[:, :, :], 0.0), writes=["carry"])
    ACT(es[:, :], prm[:, P_SINK:P_SINK + 16], AF.Exp, ["prm"], ["es"])
    for e2 in range(2):
        ACOPY(esg[64 * e2:64 * e2 + 64, :].rearrange("p (g j) -> p g j", g=4),
              es[64 * e2:64 * e2 + 64, :].rearrange("p (g j e) -> p g j e", g=4, j=2, e=2)[:, :, :, e2], ["es"], ["esg"])

    wstate = {"issued": 0}
    U_CONV = ["in%d" % f for f in range(8)] + ["out0", "out1"]
    U_FFN = [["gu%d_%d" % (l, j) for j in range(11)] + ["dn%d_%d" % (l, m) for m in range(8)] for l in range(2)]
    U_ATT = ["v", "k", "q0", "q1", "o0", "o1"]
    NT = len(TILES)
    if dbg == "L0":
        seq = (U_CONV + U_FFN[0]) * NT
    else:
        seq = U_CONV + U_FFN[0]
        for i in range(NT):
            nx_ = i + 1 < NT
            if i % 2 == 1:
                seq = seq + U_ATT + (U_CONV if nx_ else []) + U_FFN[1] + (U_FFN[0] if nx_ else [])
            else:
                seq = seq + (U_CONV if nx_ else []) + U_ATT + (U_FFN[0] if nx_ else []) + U_FFN[1]

    def wnext(name):
        j = cnt["w"]
        cnt["w"] = j + 1
        assert seq[j] == name, (seq[j], name)
        while wstate["issued"] < min(len(seq), j + NS):
            i = wstate["issued"]
            off, ncol = UOFF[seq[i]]
            sl = i % NS
            hold = [("h0", k) for k in range(8)] if (1 <= i < NS and dbg != "attn") else []
            DMA("pool", ring[:, sl, 0:ncol], wst.ap()[:, off:off + ncol], hold, [("ring", sl)])
            wstate["issued"] = i + 1
        sl = j % NS
        return ("ring", sl), sl

    def gcol(base, k):
        return prm[:, base + k:base + k + 1]

    def pipeline(jobs, s1, s2):
        prev = None
        for j in jobs:
            st_ = s1(j)
            if prev is not None:
                s2(*prev)
            prev = (j, st_)
        if prev is not None:
            s2(*prev)

    def rstats(so, n, rbuf, runit):
        pu, p = nb()
        for k in range(8):
            MM(p[:, 0:n], ones[:, :], sq[:, k, so:so + n], k == 0, k == 7, [("sq", k), "ones"], [pu])
        ACT(rbuf[:, so:so + n], p[:, 0:n], AF.Ln, [pu, "epsb"], [runit], bias=epsb[:, 0:1], scale=1.0 / D)
        ACT(rbuf[:, so:so + n], rbuf[:, so:so + n], AF.Exp, [runit], [runit], scale=-0.5)

    def norm_apply(h, hname, so, n, gb, ob, oname, ooff, rbuf, runit):
        for k in range(8):
            STT(ob[:, k, so - ooff:so - ooff + n], h[:, k, so:so + n], gcol(gb, k), rbuf[:, so:so + n], ALU.mult, ALU.mult,
                [(hname, k), "prm", runit], [(oname, k)])

    def prenorm(h, hname, subs, specs, late=()):
        ri = rot("rt", 2)
        rbuf, runit = rtb[ri], ("rt", ri)
        for (so, n) in subs:
            for k in range(8):
                ACT(sq[:, k, so:so + n], h[:, k, so:so + n], AF.Square, [(hname, k)], [("sq", k)])
            rstats(so, n, rbuf, runit)
            for (gb, ob, oname, ooff) in specs:
                norm_apply(h, hname, so, n, gb, ob, oname, ooff, rbuf, runit)
        for (gb, ob, oname, ooff, so, n) in late:
            norm_apply(h, hname, so, n, gb, ob, oname, ooff, rbuf, runit)

    def proj_postnorm(h, hname, subs, src, sname, nk, unames, per_unit, gbase, to_mix=False, hook=None):
        wu = sl = None
        for m in range(8):
            if hook is not None:
                hook(("dn", m))
            if m % per_unit == 0:
                wu, sl = wnext(unames[m // per_unit])
            ml = m % per_unit
            for (so, n) in subs:
                pu, p = nb()
                for k in range(nk):
                    c0 = (ml * nk + k) * 128
                    MM(p[:, 0:n], ring[:, sl, c0:c0 + 128], src[:, k, so:so + n], k == 0, k == nk - 1, [wu, (sname, k)], [pu])
                ACOPY(mix[:, m, so:so + n], p[:, 0:n], [pu], [("mix", m)])
                ACT(sq[:, m, so:so + n], p[:, 0:n], AF.Square, [pu], [("sq", m)])
        ri = rot("rt", 2)
        rbuf, runit = rtb[ri], ("rt", ri)
        for (so, n) in subs:
            rstats(so, n, rbuf, runit)
            for k in range(8):
                ti_ = rot("tmp", 3)
                t = tmpf[ti_]
                STT(t[:, 0:n], mix[:, k, so:so + n], gcol(gbase, k), rbuf[:, so:so + n], ALU.mult, ALU.mult,
                    [("mix", k), "prm", runit], [("tmp", ti_)])
                if to_mix:
                    TT(mix[:, k, so:so + n], h[:, k, so:so + n], t[:, 0:n], ALU.add, [("tmp", ti_), (hname, k), ("mix", k)], [("mix", k)])
                else:
                    TT(h[:, k, so:so + n], h[:, k, so:so + n], t[:, 0:n], ALU.add, [("tmp", ti_), (hname, k)], [(hname, k)])

    def ffn_pre(h, hname, subs, g_pre, xb, xname, xoff):
        prenorm(h, hname, subs, [(g_pre, xb, xname, xoff)])

    def ffn_main(h, hname, subs, layer, g_post, xb, xname, xoff, to_mix=False, hook=None):
        for j in range(11):
            if hook is not None:
                hook(("gu", j))
            wu, sl = wnext("gu%d_%d" % (layer, j))
            for fl in range(2):
                f = 2 * j + fl
                for (so, n) in subs:
                    pgu, pg = nb()
                    puu, pup = nb()
                    for gi, (pp, ppu) in enumerate(((pg, pgu), (pup, puu))):
                        for k in range(8):
                            c0 = ((fl * 2 + gi) * 8 + k) * 128
                            MM(pp[:, 0:n], ring[:, sl, c0:c0 + 128], xb[:, k, so - xoff:so - xoff + n], k == 0, k == 7, [wu, (xname, k)], [ppu])
                    si = rot("sg", 2)
                    sg = sgb[si]
                    ACT(sg[:, 0:n], pg[:, 0:n], AF.Silu, [pgu], [("sg", si)])
                    TT(aT[:, f, so:so + n], pup[:, 0:n], sg[:, 0:n], ALU.mult, [puu, ("sg", si)], [("aT", f)])
        proj_postnorm(h, hname, subs, aT, "aT", FC, ["dn%d_%d" % (layer, m) for m in range(8)], 1, g_post, to_mix=to_mix, hook=hook)

    def conv_pre(h, hname, T):
        prenorm(h, hname, T["subs"], [(P_A_PRE, xn, "xn", 0)])

    def conv_main(h, hname, T, mid=None):
        subs = T["subs"]
        n0 = T["n0"]
        for f in range(8):
            if f == 4 and mid is not None:
                mid()
            wu, sl = wnext("in%d" % f)
            ACOPY(cu[:, 0:2], carry[:, f, :], ["carry"], ["cu"])
            for (so, n) in subs:
                banks = [nb() for _ in range(3)]
                for j, (pu, p) in enumerate(banks):
                    for k in range(8):
                        c0 = k * 384 + j * 128
                        MM(p[:, 0:n], ring[:, sl, c0:c0 + 128], xn[:, k, so:so + n], k == 0, k == 7, [wu, ("xn", k)], [pu])
                (pbu, pb), (pcu, pc), (puu, pup) = banks
                ui = rot("usb", 2)
                ub = usb[ui]
                ACOPY(ub[:, 0:n], pup[:, 0:n], [puu], [("usb", ui)])
                TT(cu[:, 2 + so:2 + so + n], pc[:, 0:n], ub[:, 0:n], ALU.mult, [pcu, ("usb", ui)], ["cu"])
                t0i = rot("tmp", 3)
                t1i = rot("tmp", 3)
                t0, t1 = tmpf[t0i], tmpf[t1i]
                TSM(t0[:, 0:n], cu[:, 2 + so:2 + so + n], gcol(P_CW + 16, f), ["cu", "prm"], [("tmp", t0i)])
                STT(t1[:, 0:n], cu[:, 1 + so:1 + so + n], gcol(P_CW + 8, f), t0[:, 0:n], ALU.mult, ALU.add,
                    ["cu", "prm", ("tmp", t0i)], [("tmp", t1i)])
                STT(t0[:, 0:n], cu[:, so:so + n], gcol(P_CW, f), t1[:, 0:n], ALU.mult, ALU.add,
                    ["cu", "prm", ("tmp", t1i)], [("tmp", t0i)])
                TT(yT[:, f, so:so + n], pb[:, 0:n], t0[:, 0:n], ALU.mult, [pbu, ("tmp", t0i)], [("yT", f)])
            ACOPY(carry[:, f, :], cu[:, n0:n0 + 2], ["cu"], ["carry"])
        proj_postnorm(h, hname, subs, yT, "yT", 8, ["out0", "out1"], 4, P_A_POST)

    def rope_evac(pu, p, n, tab0, dst, dunits):
        ki = rot("kraw", 2)
        kr = kraw[ki]
        ACOPY(kr[:, 0:n], p[:, 0:n], [pu], [("kraw", ki)])
        p2u, p2 = nb()
        MM(p2[:, 0:n], rm[:, :], kr[:, 0:n], True, True, ["rm", ("kraw", ki)], [p2u])
        t0i = rot("tmp", 3)
        t1i = rot("tmp", 3)
        t0, t1 = tmpf[t0i], tmpf[t1i]
        TT(t0[:, 0:n], p[:, 0:n], ropet[:, 0, tab0:tab0 + n], ALU.mult, [pu, "ropet", ("kraw", ki)], [("tmp", t0i)])
        TT(t1[:, 0:n], p2[:, 0:n], ropet[:, 1, tab0:tab0 + n], ALU.mult, [p2u, "ropet"], [("tmp", t1i)])
        TT(dst, t0[:, 0:n], t1[:, 0:n], ALU.add, [("tmp", t0i), ("tmp", t1i)], dunits)

    def attn_pre(h, hname, T, ti, xkvb, xkvn):
        kv0, q0, nkb = T["kv0"], T["q0"], T["nkb"]
        kvl0 = T["x0"] + kv0 - 2
        nkv = nkb * 128
        DMA("sp", ropet[:, :, 0:nkv], rope_d.ap()[:, :, kvl0:kvl0 + nkv], [], ["ropet"])
        if ti > 0:
            ACOPY(KT[:, :, 0:128], KT[:, :, 512:640], [("KT", g, 4) for g in range(4)], [("KT", g, 0) for g in range(4)])
            ACOPY(Vb[:, 0, :, :], Vb[:, 4, :, :], [("Vb", 4)], [("Vb", 0)])
        ri = rot("rt", 2)
        rbuf, runit = rtb[ri], ("rt", ri)
        for (so, n) in T["subs"]:
            for k in range(8):
                ACT(sq[:, k, so:so + n], h[:, k, so:so + n], AF.Square, [(hname, k)], [("sq", k)])
            rstats(so, n, rbuf, runit)
            norm_apply(h, hname, so, n, P_KVN, xkvb, xkvn, 0, rbuf, runit)
        return rbuf, runit

    def attn_pre_q(h, hname, T, rr):
        q0 = T["q0"]
        norm_apply(h, hname, q0, 512, P_B_PRE, xnB, "xnB", q0, rr[0], rr[1])

    def attn_main(h, hname, T, ti, xkv, xkvn, mid=None, pre_q=None):
        n0, kv0, q0, nkb = T["n0"], T["kv0"], T["q0"], T["nkb"]
        ks0 = 5 - nkb
        tq0 = q0 - kv0
        if dbg == "attn":
            return attn_core(T, ti, int(ATTN_DBG[0]), int(ATTN_DBG[1]))
        wu, sl = wnext("v")
        for bi in range(nkb):
            pu, p = nb()
            t0 = kv0 + bi * 128
            for k in range(8):
                MM(p[:, 0:256], xkv[:, k, t0:t0 + 128], ring[:, sl, k * 256:(k + 1) * 256], k == 0, k == 7, [wu, (xkvn, k)], [pu])
            vs = ks0 + bi
            ACOPY(Vb[:, vs, :, :], p[:, 0:256].rearrange("p (g d) -> p g d", g=4), [pu], [("Vb", vs)])
        wu, sl = wnext("k")
        kjobs = []
        for g in range(4):
            for (po, pn) in T["kvp"]:
                kjobs.append((g, po, pn, wu, sl))

        def k_s1(job):
            g, po, pn, wu_, sl_ = job
            pu, p = nb()
            for k in range(8):
                c0 = (g * 8 + k) * 128
                MM(p[:, 0:pn], ring[:, sl_, c0:c0 + 128], xkv[:, k, kv0 + po:kv0 + po + pn], k == 0, k == 7, [wu_, (xkvn, k)], [pu])
            return (pu, p)

        def k_s2(job, st_):
            g, po, pn, wu_, sl_ = job
            pu, p = st_
            c1 = ks0 * 128 + po
            rope_evac(pu, p, pn, po, KT[:, g, c1:c1 + pn], [("KT", g, s_) for s_ in range(5)])

        pipeline(kjobs, k_s1, k_s2)
        if pre_q is not None:
            pre_q()
        if mid is not None:
            mid()
        def q_s1(c):
            if c % 4 == 0:
                qst["w"] = wnext("q%d" % (c // 4))
            wu_, sl_ = qst["w"]
            pu, p = nb()
            for k in range(8):
                c0 = ((c % 4) * 8 + k) * 128
                MM(p[:, 0:512], ring[:, sl_, c0:c0 + 128], xnB[:, k, 0:512], k == 0, k == 7, [wu_, ("xnB", k)], [pu])
            return (pu, p)

        def q_s2(c, st_):
            pu, p = st_
            rope_evac(pu, p, 512, tq0, aT[:, 8 + c, 0:512], [("aT", 8 + c)])

        qst = {}
        pipeline(list(range(8)), q_s1, q_s2)
        attn_core(T, ti, 4, 4)
        proj_postnorm(h, hname, [(q0, 512)], yT, "yT", 8, ["o0", "o1"], 4, P_B_POST)

    def attn_core(T, ti, nqb, ng):
        q0 = T["q0"]
        def a_s1(job):
            qb, g = job
            qc = qb * 128
            su, bi0 = nb2()
            kslots = (qb, qb + 1)
            for kbi, kslot in enumerate(kslots):
                for e2 in range(2):
                    MM(pst[:, bi0 + e2, kbi * 256:(kbi + 1) * 256].rearrange("p (j q) -> p j q", j=2),
                       KT[64 * e2:64 * e2 + 64, g, kslot * 128:(kslot + 1) * 128],
                       aT[64 * e2:64 * e2 + 64, 8 + 2 * g:8 + 2 * g + 2, qc:qc + 128], True, True,
                       [("KT", g, kslot), ("aT", 8 + 2 * g), ("aT", 9 + 2 * g)], su)
            pms = []
            for kbi, kslot in enumerate(kslots):
                pi = rot("pt", 2)
                ptile = aT[:, 16 + pi, 0:512]
                ACT(ptile.rearrange("p (e q) -> p e q", e=2), pst[:, bi0:bi0 + 2, kbi * 256:(kbi + 1) * 256], AF.Exp,
                    su, [("aT", 16 + pi)], scale=0.125)
                mi = 2 if (kbi == 0 and ti == 0 and qb == 0) else kbi
                mpi = rot("pm", 4)
                pmtile = aT[:, 18 + mpi, 0:512]
                TT(pmtile, ptile, msk[:, mi, :], ALU.mult, [("aT", 16 + pi), "msk"], [("aT", 18 + mpi)])
                pms.append((pmtile, ("aT", 18 + mpi), kslot))
            return pms

        def a_s2(job, pms):
            qb, g = job
            qc = qb * 128
            ou, ob = nb()
            du, db = nb()
            for kbi, (pmtile, pmu, kslot) in enumerate(pms):
                for e2 in range(2):
                    MM(ob[64 * e2:64 * e2 + 64, 0:256], Vb[:, kslot, g, :], pmtile[:, e2 * 256:(e2 + 1) * 256], kbi == 0, kbi == 1,
                       [("Vb", kslot), pmu], [ou])
                for e2 in range(2):
                    MM(db[64 * e2:64 * e2 + 64, 0:256], ones[:, 0:64], pmtile[:, e2 * 256:(e2 + 1) * 256], kbi == 0, kbi == 1,
                       ["ones", pmu], [du])
            half = g % 2
            if half == 0:
                ti_ = rot("tmp", 3)
                pst_["pair"] = (ti_, tmpf[ti_])
                pst_["prev"] = (ou, ob, g)
            ti_, rp = pst_["pair"]
            esap = bass.AP(esg, 2 * g, [[8, 128], [1, 2], [0, 128]])
            r3 = rp[:, half * 256:(half + 1) * 256].rearrange("p (j q) -> p j q", j=2)
            TT(r3, db[:, 0:256].rearrange("p (j q) -> p j q", j=2), esap, ALU.add, [du, "esg"], [("tmp", ti_)])
            if half == 1:
                ACT(rp[:, 0:512], rp[:, 0:512], AF.Ln, [("tmp", ti_)], [("tmp", ti_)])
                ACT(rp[:, 0:512], rp[:, 0:512], AF.Exp, [("tmp", ti_)], [("tmp", ti_)], scale=-1.0)
                for hf, (ou_, ob_, g_) in enumerate((pst_["prev"], (ou, ob, g))):
                    TT(yT[:, 2 * g_:2 * g_ + 2, q0 + qc:q0 + qc + 128], ob_[:, 0:256].rearrange("p (j q) -> p j q", j=2),
                       rp[:, hf * 256:(hf + 1) * 256].rearrange("p (j q) -> p j q", j=2), ALU.mult,
                       [ou_, ("tmp", ti_)], [("yT", 2 * g_), ("yT", 2 * g_ + 1)])

        pst_ = {}
        pipeline([(qb, g) for qb in range(nqb) for g in range(ng)], a_s1, a_s2)

    def load_x(ti):
        T = TILES[ti]
        h, hname = hb[ti % 2], "h%d" % (ti % 2)
        for k in range(8):
            DMA("pool" if ti > 0 else "sp", h[:, k, 0:T["n0"]], xT.ap()[:, k, T["x0"]:T["x0"] + T["n0"]], [], [(hname, k)])

    def hb_of(ti):
        return hb[ti % 2], "h%d" % (ti % 2)

    def out_dma(ti, from_mix=False):
        h, hname = hb_of(ti)
        q0 = TILES[ti]["q0"]
        if from_mix:
            for k in range(8):
                DMA("sp", oT.ap()[:, k, ti * 512:(ti + 1) * 512], mix[:, k, q0:q0 + 512], [("mix", k)], [("out", ti, k)])
                outs.append(("out", ti, k))
            return
        else:
            DMA("sp", oT.ap()[:, :, ti * 512:(ti + 1) * 512], h[:, :, q0:q0 + 512], [(hname, k) for k in range(8)], [("out", ti)])
        outs.append(("out", ti))

    outs = []
    if dbg == "attn":
        load_x(0)
        T = TILES[0]
        DMA("pool", aT[:, :, 0:512], wst.ap()[:, 0:FC * 512].rearrange("p (c n) -> p c n", c=FC), [], [("aT", f) for f in range(FC)])
        DMA("pool", KT[:, :, :], wst.ap()[:, 20000:20000 + 2560].rearrange("p (c n) -> p c n", c=4), [], [("KT", g, sl_) for g in range(4) for sl_ in range(5)])
        DMA("pool", Vb[:, :, :, :], wst.ap()[:, 30000:30000 + 1280].rearrange("p (a b c) -> p a b c", a=5, b=4), [], [("Vb", i) for i in range(5)])
        attn_main(hb[0], "h0", T, 0, aT, "aT")
        for k in range(8):
            S.op("dve", lambda e, k=k: e.tensor_copy(out=hb[0][:, k, 0:512], in_=yT[:, k, 130:642]), reads=[("yT", k), ("h0", k)], writes=[("h0", k)])
        DMA("sp", oT.ap()[:, :, 0:512], hb[0][:, :, 0:512], [("h0", k) for k in range(8)], [("out", 0)])
        S.op("sp", lambda e: e.nop(), reads=[("out", 0)])
        S.emit()
        return nc
    if dbg == "L0":
        for ti, T in enumerate(TILES):
            h, hname = hb_of(ti)
            load_x(ti)
            conv_pre(h, hname, T)
            conv_main(h, hname, T)
            ffn_pre(h, hname, T["subs"], P_F_PRE0, xn, "xn", 0)
            ffn_main(h, hname, T["subs"], 0, P_F_POST0, xn, "xn", 0)
            out_dma(ti)
    else:
        def make_prefetch(tn):
            Tn = TILES[tn]
            hn, hnname = hb_of(tn)
            (so, n), = Tn["subs"]
            st = {}

            def hook(m):
                if m == ("gu", 0):
                    load_x(tn)
                elif m == ("gu", 7):
                    for k in range(8):
                        ACT(sq[:, k, so:so + n], hn[:, k, so:so + n], AF.Square, [(hnname, k)], [("sq", k)])
                elif m == ("gu", 9):
                    ri = rot("rt", 2)
                    st["r"] = (rtb[ri], ("rt", ri))
                    rstats(so, n, *st["r"])
                elif m == ("dn", 0):
                    norm_apply(hn, hnname, so, n, P_A_PRE, xn, "xn", 0, *st["r"])
            return hook

        pend = []

        def flush():
            while pend:
                pend.pop(0)()

        def gu1(extra=None):
            def hook(m):
                if m == ("gu", 5):
                    flush()
                if extra is not None:
                    extra(m)
            return hook

        def stage_C(tn):
            Tn = TILES[tn]
            hn, hnname = hb_of(tn)
            conv_main(hn, hnname, Tn, mid=flush)
            pend.append(lambda: ffn_pre(hn, hnname, Tn["subs"], P_F_PRE0, xn, "xn", 0))

        def stage_F0(tn, extra=None):
            Tn = TILES[tn]
            hn, hnname = hb_of(tn)
            flush_needed = [p for p in pend]
            ffn_main(hn, hnname, Tn["subs"], 0, P_F_POST0, xn, "xn", 0, hook=gu1(extra))

        def stage_A(ti, xkvb, xkvn, mid_extra=None, pre_q=None):
            T = TILES[ti]
            h, hname = hb_of(ti)

            def mid():
                flush()
                if mid_extra is not None:
                    mid_extra()
            attn_main(h, hname, T, ti, xkvb, xkvn, mid=mid, pre_q=pre_q)
            pend.append(lambda: ffn_pre(h, hname, [(T["q0"], 512)], P_F_PRE1, xnB, "xnB", T["q0"]))

        def stage_F1(ti, pre_out=None):
            T = TILES[ti]
            h, hname = hb_of(ti)
            ffn_main(h, hname, [(T["q0"], 512)], 1, P_F_POST1, xnB, "xnB", T["q0"], to_mix=True, hook=gu1())
            if pre_out is not None:
                pre_out()
            out_dma(ti, from_mix=True)

        h, hname = hb_of(0)
        load_x(0)
        conv_pre(h, hname, TILES[0])
        stage_C(0)
        flush()
        stage_F0(0, extra=make_prefetch(1))
        st2 = {}

        def c2_0():
            hh, hhn = hb_of(0)
            r_ = attn_pre(hh, hhn, TILES[0], 0, aT, "aT")
            attn_pre_q(hh, hhn, TILES[0], r_)
        pend.append(c2_0)
        for ti in range(NT):
            T = TILES[ti]
            h, hname = hb_of(ti)
            nx = ti + 1 < NT
            if ti % 2 == 1:
                rr = st2["rr"]
                xk = (yT, "yT")
                mid_extra = None
                if nx:
                    mid_extra = (lambda tn=ti + 1: conv_pre(*hb_of(tn), TILES[tn]))
                stage_A(ti, *xk, mid_extra=mid_extra, pre_q=(lambda h=h, hname=hname, T=T, rr=rr: attn_pre_q(h, hname, T, rr)))
                if nx:
                    stage_C(ti + 1)
                else:
                    flush()
                stage_F1(ti)
                if nx:
                    stage_F0(ti + 1, extra=(make_prefetch(ti + 2) if ti + 2 < NT else None))
                    if ti + 1 < NT:
                        def c2(tn=ti + 1):
                            hh, hhn = hb_of(tn)
                            r_ = attn_pre(hh, hhn, TILES[tn], tn, aT, "aT")
                            attn_pre_q(hh, hhn, TILES[tn], r_)
                        pend.append(c2)
            else:
                if nx:
                    stage_C(ti + 1)
                else:
                    flush()
                stage_A(ti, aT, "aT")
                if nx:
                    stage_F0(ti + 1)

                    def c2kv(tn=ti + 1):
                        st2["rr"] = attn_pre(*hb_of(tn), TILES[tn], tn, yT, "yT")
                    pend.append(c2kv)
                else:
                    flush()
                pre_out = (lambda tn=ti + 2: load_x(tn)) if ti + 2 < NT else None
                stage_F1(ti, pre_out=pre_out)
        flush()
    S.op("sp", lambda e: e.nop(), reads=outs)
    S.emit()
    return nc


def _fm(vec):
    return np.ascontiguousarray(np.asarray(vec, np.float32).reshape(8, 128).T)


def _wrows(w):
    K, N = w.shape
    return np.asarray(w, np.float32).reshape(K // 128, 128, N).transpose(1, 0, 2)


def _build_wstream(a_w_in, a_w_out, gu, dn, w_kv, b_w_q, b_w_o):
    ws = np.empty((128, TOTC), np.float32)

    def put(name, arr):
        off, ncol = UOFF[name]
        ws[:, off:off + ncol] = arr.reshape(128, ncol)

    win = _wrows(a_w_in[0])
    for f in range(8):
        blk = np.stack([win[:, :, j * 1024 + f * 128:j * 1024 + (f + 1) * 128] for j in range(3)], axis=2)
        put("in%d" % f, blk)

    def outlike(w, names):
        wr = _wrows(w)
        for j in range(2):
            blk = np.stack([wr[:, :, (4 * j + ml) * 128:(4 * j + ml + 1) * 128] for ml in range(4)], axis=1)
            put(names[j], blk)

    outlike(a_w_out[0], ["out0", "out1"])
    outlike(b_w_q[0], ["q0", "q1"])
    outlike(b_w_o[0], ["o0", "o1"])
    for l in range(2):
        g = _wrows(gu[l])
        for j in range(11):
            parts = []
            for fl in range(2):
                f = 2 * j + fl
                for gi in range(2):
                    parts.append(g[:, :, gi * DFF + f * 128:gi * DFF + (f + 1) * 128])
            put("gu%d_%d" % (l, j), np.stack(parts, axis=1))
        d = _wrows(dn[l])
        for m in range(8):
            put("dn%d_%d" % (l, m), np.ascontiguousarray(d[:, :, m * 128:(m + 1) * 128]))
    kv = _wrows(w_kv)
    kparts = []
    for g4 in range(4):
        hk = kv[:, :, g4 * 64:(g4 + 1) * 64]
        kparts.append(np.concatenate([hk, hk], axis=2))
    put("k", np.stack(kparts, axis=1))
    put("v", np.ascontiguousarray(kv[:, :, 256:512]))
    return ws


_CACHE = {}


def kernel(x, a_pre_norm, a_w_in, a_conv_w, a_w_out, a_post_norm,
           ffn_pre_norm, ffn_w_gate_up, ffn_w_down, ffn_post_norm,
           kv_norm, w_kv, b_pre_norm, b_w_q, b_sinks, b_w_o, b_post_norm, _dbg=None):
    x = np.asarray(x, np.float32)
    prm = np.zeros((128, NPRM), np.float32)
    for base, vec in ((P_A_PRE, a_pre_norm[0]), (P_A_POST, a_post_norm[0]), (P_F_PRE0, ffn_pre_norm[0]),
                      (P_F_POST0, ffn_post_norm[0]), (P_KVN, kv_norm), (P_B_PRE, b_pre_norm[0]), (P_B_POST, b_post_norm[0]),
                      (P_F_PRE1, ffn_pre_norm[1]), (P_F_POST1, ffn_post_norm[1])):
        prm[:, base:base + 8] = _fm(vec)
    for j in range(3):
        prm[:, P_CW + 8 * j:P_CW + 8 * j + 8] = _fm(np.asarray(a_conv_w)[0, j])
    prm[:, P_SINK:P_SINK + 16] = np.asarray(b_sinks, np.float32)[0][None, :]
    wst = _build_wstream(np.asarray(a_w_in), np.asarray(a_w_out), np.asarray(ffn_w_gate_up), np.asarray(ffn_w_down),
                         np.asarray(w_kv), np.asarray(b_w_q), np.asarray(b_w_o))
    rm = np.zeros((128, 128), np.float32)
    for hh in range(2):
        for d in range(8):
            rm[hh * 64 + d + 8, hh * 64 + d] = -1.0
            rm[hh * 64 + d, hh * 64 + d + 8] = 1.0
    kk = np.arange(128)[:, None]
    qq = np.arange(128)[None, :]
    mprev = np.tile((kk > qq).astype(np.float32), (1, 4))
    mcur = np.tile((kk <= qq).astype(np.float32), (1, 4))
    inv_freq = (np.float32(ROPE_THETA) ** (-np.arange(0, 16, 2, dtype=np.float32) / np.float32(16))).astype(np.float32)
    in_maps = []
    for c in range(NCORES):
        b, half = c // 2, c % 2
        s0 = half * TOK
        xs = np.zeros((XT, D), np.float32)
        lo = s0 - 130
        src_lo = max(lo, 0)
        xs[src_lo - lo:] = x[b, src_lo:s0 + TOK]
        xTc = np.ascontiguousarray(xs.reshape(XT, 8, 128).transpose(2, 1, 0))
        pos = (s0 - 128 + np.arange(TOK + 128)).astype(np.float32)
        ang = pos[:, None] * inv_freq[None, :]
        cosv, sinv = np.cos(ang).astype(np.float32), np.sin(ang).astype(np.float32)
        rope = np.zeros((128, 2, TOK + 128), np.float32)
        rope[:, 0, :] = 1.0
        for hh in range(2):
            for d in range(16):
                rope[hh * 64 + d, 0, :] = cosv[:, d % 8]
                rope[hh * 64 + d, 1, :] = sinv[:, d % 8]
        msk = np.stack([mprev, mcur, mprev if half == 1 else np.zeros_like(mprev)], axis=1)
        in_maps.append({"xT": xTc, "wst": wst, "prm": prm, "rope": rope, "msk": np.ascontiguousarray(msk), "rm": rm})
    key = _dbg
    if key not in _CACHE:
        _CACHE[key] = build_program(_dbg)
    nc = _CACHE[key]
    res = run_bass_kernel_spmd(nc, in_maps, core_ids=list(range(NCORES)))
    out = np.empty((BATCH, SEQ, D), np.float32)
    for c in range(NCORES):
        b, half = c // 2, c % 2
        o = res.results[c]["oT"]
        out[b, half * TOK:(half + 1) * TOK, :] = o.transpose(2, 1, 0).reshape(TOK, D)
    return out
```
